# Optimizing a Trainium2 kernel written in Bass

```python
import jax, jax.numpy as jnp
from jax import lax
import numpy as np

D_MODEL = 1024
BATCH = 2
SEQ = 8192
DEPTH = 4
DEC_BATCH = 8
DEC_SEQ = 4096
PAST_LEN = 128

HEAD_DIM = 64
ROPE_THETA = 10000.0
NORM_EPS = 1e-6
D_FF = 2816
N_MEM = 256
MEM_HEADS = 4
MEM_WIDTH = MEM_HEADS * HEAD_DIM
A_Q_HEADS = 12
A_KV_HEADS = 4
A_GROUP = A_Q_HEADS // A_KV_HEADS
WINDOW = 128
BLOCK = 128
A_Q_W = A_Q_HEADS * HEAD_DIM
A_KV_W = A_KV_HEADS * HEAD_DIM
A_IN = A_Q_W + 2 * A_KV_W + MEM_WIDTH
B_HEADS = 12
Q_LORA = 384
KV_LORA = 256
QK_NOPE = 64
QK_ROPE = 32
V_HEAD = 64
B_QK = QK_NOPE + QK_ROPE
B_IN = Q_LORA + KV_LORA + QK_ROPE + MEM_WIDTH
Q_BLOCK = 128
MIX_WIDTH = A_Q_W + MEM_WIDTH
N_A_LAYERS = (DEPTH + 1) // 2
N_B_LAYERS = DEPTH // 2
NEG = -1e30

kernel_name = "hybrid_swa_mla_memory_macaron_encoder"


def rmsnorm(x, g):
    xf = x.astype(jnp.float32)
    y = xf * lax.rsqrt(jnp.mean(xf * xf, axis=-1, keepdims=True) + NORM_EPS)
    return (y * g.astype(jnp.float32)).astype(x.dtype)


def rope_tables(seq, dim):
    inv = 1.0 / (ROPE_THETA ** (jnp.arange(0, dim, 2, dtype=jnp.float32) / dim))
    ang = jnp.arange(seq, dtype=jnp.float32)[:, None] * inv[None, :]
    return jnp.cos(ang), jnp.sin(ang)


def apply_rope(x, cos, sin):
    xf = x.astype(jnp.float32)
    x1, x2 = jnp.split(xf, 2, axis=-1)
    c = cos[None, :, None, :]
    s = sin[None, :, None, :]
    return jnp.concatenate([x1 * c - x2 * s, x2 * c + x1 * s], axis=-1).astype(x.dtype)


def swiglu(x, g, w_gu, w_down):
    gate, up = jnp.split(rmsnorm(x, g) @ w_gu, 2, axis=-1)
    return (jax.nn.silu(gate) * up) @ w_down


def windowed_gqa(q, k, v, sink):
    b, s, _, _ = q.shape
    nb = s // BLOCK
    qb = q.reshape(b, nb, BLOCK, A_KV_HEADS, A_GROUP, HEAD_DIM)

    def band(t):
        tp = jnp.pad(t, ((0, 0), (WINDOW, WINDOW), (0, 0), (0, 0)))
        tb = tp.reshape(b, nb + 2, BLOCK, A_KV_HEADS, HEAD_DIM)
        return jnp.concatenate([tb[:, :-2], tb[:, 1:-1], tb[:, 2:]], axis=2)

    kb, vb = band(k), band(v)
    sc = jnp.einsum('bnqgrd,bnkgd->bngrqk', qb, kb,
                    preferred_element_type=jnp.float32) * (HEAD_DIM ** -0.5)
    qpos = jnp.arange(nb)[:, None] * BLOCK + jnp.arange(BLOCK)[None, :]
    kpos = jnp.arange(nb)[:, None] * BLOCK - WINDOW + jnp.arange(3 * BLOCK)[None, :]
    valid = ((jnp.abs(qpos[:, :, None] - kpos[:, None, :]) <= WINDOW)
             & (kpos >= 0)[:, None, :] & (kpos < s)[:, None, :])
    sc = jnp.where(valid[None, :, None, None], sc, NEG)
    sk = sink.astype(jnp.float32).reshape(A_KV_HEADS, A_GROUP)[None, None, :, :, None, None]
    m = jnp.maximum(jnp.max(sc, axis=-1, keepdims=True), sk)
    p = jnp.exp(sc - m)
    p = p / (jnp.sum(p, axis=-1, keepdims=True) + jnp.exp(sk - m))
    o = jnp.einsum('bngrqk,bnkgd->bnqgrd', p.astype(v.dtype), vb)
    return o.reshape(b, s, A_Q_W)


def dense_attention_blocks(q, k, v):
    b, s, h, dk = q.shape
    dv = v.shape[-1]
    nb = s // Q_BLOCK
    qb = q.reshape(b, nb, Q_BLOCK, h, dk).transpose(1, 0, 2, 3, 4)
    scale = dk ** -0.5

    def one_block(qblk):
        sc = jnp.einsum('bqhd,bkhd->bhqk', qblk, k, preferred_element_type=jnp.float32) * scale
        p = jax.nn.softmax(sc, axis=-1)
        return jnp.einsum('bhqk,bkhd->bqhd', p.astype(v.dtype), v)

    o = lax.map(one_block, qb)
    return o.transpose(1, 0, 2, 3, 4).reshape(b, s, h * dv)


def memory_attention(qc, mk, mv):
    b, s, _ = qc.shape
    q = qc.reshape(b, s, MEM_HEADS, HEAD_DIM)
    k = mk.reshape(b, N_MEM, MEM_HEADS, HEAD_DIM)
    vv = mv.reshape(b, N_MEM, MEM_HEADS, HEAD_DIM)
    sc = jnp.einsum('bqhd,bkhd->bhqk', q, k, preferred_element_type=jnp.float32) * (HEAD_DIM ** -0.5)
    p = jax.nn.softmax(sc, axis=-1)
    return jnp.einsum('bhqk,bkhd->bqhd', p.astype(vv.dtype), vv).reshape(b, s, MEM_WIDTH)


def mixer_a(h, w_in, sink, cos, sin):
    b, s, _ = h.shape
    proj = h @ w_in
    q, k, v, qc = jnp.split(proj, [A_Q_W, A_Q_W + A_KV_W, A_Q_W + 2 * A_KV_W], axis=-1)
    q = apply_rope(q.reshape(b, s, A_Q_HEADS, HEAD_DIM), cos, sin)
    k = apply_rope(k.reshape(b, s, A_KV_HEADS, HEAD_DIM), cos, sin)
    v = v.reshape(b, s, A_KV_HEADS, HEAD_DIM)
    return windowed_gqa(q, k, v, sink), qc


def mixer_b(h, w_in, q_norm, w_q_up, kv_norm, w_kv_up, cos, sin):
    b, s, _ = h.shape
    proj = h @ w_in
    c_q, c_kv, k_r, qc = jnp.split(proj, [Q_LORA, Q_LORA + KV_LORA, Q_LORA + KV_LORA + QK_ROPE], axis=-1)
    q = (rmsnorm(c_q, q_norm) @ w_q_up).reshape(b, s, B_HEADS, B_QK)
    q_nope, q_rope = jnp.split(q, [QK_NOPE], axis=-1)
    q = jnp.concatenate([q_nope, apply_rope(q_rope, cos, sin)], axis=-1)
    kv = (rmsnorm(c_kv, kv_norm) @ w_kv_up).reshape(b, s, B_HEADS, QK_NOPE + V_HEAD)
    k_nope, v = jnp.split(kv, [QK_NOPE], axis=-1)
    k_rope = apply_rope(k_r[:, :, None, :], cos, sin)
    k = jnp.concatenate([k_nope, jnp.broadcast_to(k_rope, (b, s, B_HEADS, QK_ROPE))], axis=-1)
    return dense_attention_blocks(q, k, v), qc


def run_trunk(x, mem, P):
    s = x.shape[1]
    cos_a, sin_a = rope_tables(s, HEAD_DIM)
    cos_b, sin_b = rope_tables(s, QK_ROPE)
    for i in range(DEPTH):
        x = x + 0.5 * swiglu(x, P['ffn1_norm'][i], P['ffn1_w_gu'][i], P['ffn1_w_down'][i])
        h = rmsnorm(x, P['mix_norm'][i])
        mk, mv = jnp.split(rmsnorm(mem, P['mem_norm'][i]) @ P['w_mem_kv'][i], 2, axis=-1)
        j = i // 2
        if i % 2 == 0:
            local, qc = mixer_a(h, P['a_w_in'][j], P['a_sink'][j], cos_a, sin_a)
        else:
            local, qc = mixer_b(h, P['b_w_in'][j], P['b_q_norm'][j], P['b_w_q_up'][j],
                                P['b_kv_norm'][j], P['b_w_kv_up'][j], cos_b, sin_b)
        cross = memory_attention(qc, mk, mv)
        x = x + jnp.concatenate([local, cross], axis=-1) @ P['w_o'][i]
        x = x + 0.5 * swiglu(x, P['ffn2_norm'][i], P['ffn2_w_gu'][i], P['ffn2_w_down'][i])
    return rmsnorm(x, P['final_norm'])


def setup_inputs(seed: int = 0) -> dict:
    key = jax.random.key(seed)
    ks = iter(jax.random.split(key, 32))

    def w(shape, fan_in):
        return jax.random.normal(next(ks), shape, jnp.float32) * (fan_in ** -0.5)

    def gain(shape):
        return 1.0 + 0.1 * jax.random.normal(next(ks), shape, jnp.float32)

    return {
        'x_prompt': jax.random.normal(next(ks), (BATCH, SEQ, D_MODEL), jnp.float32),
        'x_sample': jax.random.normal(next(ks), (DEC_BATCH, DEC_SEQ, D_MODEL), jnp.float32),
        'mem_prompt': jax.random.normal(next(ks), (BATCH, N_MEM, D_MODEL), jnp.float32),
        'mem_sample': jax.random.normal(next(ks), (DEC_BATCH, N_MEM, D_MODEL), jnp.float32),
        'ffn1_norm': gain((DEPTH, D_MODEL)),
        'ffn1_w_gu': w((DEPTH, D_MODEL, 2 * D_FF), D_MODEL),
        'ffn1_w_down': w((DEPTH, D_FF, D_MODEL), D_FF),
        'mix_norm': gain((DEPTH, D_MODEL)),
        'mem_norm': gain((DEPTH, D_MODEL)),
        'w_mem_kv': w((DEPTH, D_MODEL, 2 * MEM_WIDTH), D_MODEL),
        'a_w_in': w((N_A_LAYERS, D_MODEL, A_IN), D_MODEL),
        'a_sink': 0.5 * jax.random.normal(next(ks), (N_A_LAYERS, A_Q_HEADS), jnp.float32),
        'b_w_in': w((N_B_LAYERS, D_MODEL, B_IN), D_MODEL),
        'b_q_norm': gain((N_B_LAYERS, Q_LORA)),
        'b_w_q_up': w((N_B_LAYERS, Q_LORA, B_HEADS * B_QK), Q_LORA),
        'b_kv_norm': gain((N_B_LAYERS, KV_LORA)),
        'b_w_kv_up': w((N_B_LAYERS, KV_LORA, B_HEADS * (QK_NOPE + V_HEAD)), KV_LORA),
        'w_o': w((DEPTH, MIX_WIDTH, D_MODEL), MIX_WIDTH),
        'ffn2_norm': gain((DEPTH, D_MODEL)),
        'ffn2_w_gu': w((DEPTH, D_MODEL, 2 * D_FF), D_MODEL),
        'ffn2_w_down': w((DEPTH, D_FF, D_MODEL), D_FF),
        'final_norm': gain((D_MODEL,)),
    }


def reference(x_prompt, x_sample, mem_prompt, mem_sample,
              ffn1_norm, ffn1_w_gu, ffn1_w_down, mix_norm, mem_norm, w_mem_kv,
              a_w_in, a_sink, b_w_in, b_q_norm, b_w_q_up, b_kv_norm, b_w_kv_up,
              w_o, ffn2_norm, ffn2_w_gu, ffn2_w_down, final_norm):
    P = {
        'ffn1_norm': ffn1_norm, 'ffn1_w_gu': ffn1_w_gu, 'ffn1_w_down': ffn1_w_down,
        'mix_norm': mix_norm, 'mem_norm': mem_norm, 'w_mem_kv': w_mem_kv,
        'a_w_in': a_w_in, 'a_sink': a_sink,
        'b_w_in': b_w_in, 'b_q_norm': b_q_norm, 'b_w_q_up': b_w_q_up,
        'b_kv_norm': b_kv_norm, 'b_w_kv_up': b_w_kv_up,
        'w_o': w_o, 'ffn2_norm': ffn2_norm, 'ffn2_w_gu': ffn2_w_gu,
        'ffn2_w_down': ffn2_w_down, 'final_norm': final_norm,
    }
    y_prompt = run_trunk(x_prompt, mem_prompt, P)
    y_sample = run_trunk(x_sample, mem_sample, P)
    return (y_prompt, y_sample)
```

```python
import contextlib
import numpy as np
import concourse.bass as bass
import concourse.mybir as mybir
from concourse.bass_utils import run_bass_kernel_spmd

F32 = mybir.dt.float32
BF16 = mybir.dt.bfloat16
AF = mybir.ActivationFunctionType
ALU = mybir.AluOpType

D = 1024
DC = 8
FF = 2816
FC = 22
TT = 512
NMEM = 256
EPS = 1e-6
ENGS = ("pe", "act", "dve", "pool", "sp")
N_DMA_SEMS = 40
DMA_SLOTS = {"sp": (0, 28), "pool": (28, 4), "act": (32, 8)}
EPOCH = 30000


class T:
    __slots__ = ("name", "lw", "rd")

    def __init__(self, name):
        self.name = name
        self.lw = None
        self.rd = []


def TL(name, n):
    return [T("%s%d" % (name, i)) for i in range(n)]


def _flat(x):
    out = []
    for t in x:
        if isinstance(t, (list, tuple)):
            out.extend(_flat(t))
        elif isinstance(t, DT):
            out.extend(t.cur)
        else:
            out.append(t)
    return out


class DT:
    def __init__(self, name):
        self.name = name
        self.cur = []
        self.old = []

    def w(self):
        t = T(self.name)
        self.cur.append(t)
        if self.old:
            o = self.old
            self.old = []
            return o + [t]
        return [t]

    def next_gen(self):
        self.old = self.old + self.cur
        self.cur = []


class Instr:
    __slots__ = ("eng", "idx", "fn", "deps", "is_dma", "dma_n", "signal", "cnt", "slot", "sval")

    def __init__(self, eng, idx, fn, is_dma):
        self.eng = eng
        self.idx = idx
        self.fn = fn
        self.deps = []
        self.is_dma = is_dma
        self.dma_n = None
        self.signal = False
        self.cnt = None


class Sched:
    def __init__(self, nc):
        self.nc = nc
        self.q = {e: [] for e in ENGS}
        self.n_dma = 0
        self.dmas = []
        self.seen_eng = {e: {} for e in ENGS}
        self.seen_dma = {e: set() for e in ENGS}
        self.dma_by_q = {}

    def op(self, eng, fn, reads=(), writes=(), dma=False):
        ins = Instr(eng, len(self.q[eng]), fn, dma)
        deps = {}

        def add(d, war=False):
            if d is None:
                return
            if d.eng == eng and not d.is_dma and not dma:
                if war or eng == "pe" or eng == "sp":
                    return
            if d.is_dma:
                if d.dma_n in self.seen_dma[eng]:
                    return
                deps[("dma", d.dma_n)] = d
            else:
                if self.seen_eng[eng].get(d.eng, -1) >= d.idx:
                    return
                if d.eng not in deps or deps[d.eng].idx < d.idx:
                    deps[d.eng] = d

        reads = _flat(reads)
        writes = _flat(writes)
        for t in reads:
            add(t.lw)
        for t in writes:
            add(t.lw)
            for r in t.rd:
                add(r, war=True)
        if dma:
            ins.dma_n = self.n_dma
            self.n_dma += 1
            self.dmas.append(ins)
            base, nsl = DMA_SLOTS[eng]
            lst = self.dma_by_q.setdefault(eng, [])
            k = len(lst)
            ins.slot = base + (k % nsl)
            ins.sval = 16 * (k // nsl + 1)
            if k >= nsl:
                prev = lst[k - nsl]
                if prev.dma_n not in self.seen_dma[eng]:
                    deps[("dma", prev.dma_n)] = prev
            lst.append(ins)
        ins.deps = list(deps.values())
        for d in ins.deps:
            d.signal = True
            if d.is_dma:
                self.seen_dma[eng].add(d.dma_n)
            else:
                self.seen_eng[eng][d.eng] = d.idx
        for t in reads:
            t.rd.append(ins)
        for t in writes:
            t.lw = ins
            t.rd = []
        self.q[eng].append(ins)
        return ins

    def emit(self):
        nc = self.nc
        nep = {}
        for e in ENGS:
            c = 0
            for ins in self.q[e]:
                if not ins.is_dma and ins.signal:
                    ins.cnt = c
                    c += 1
            nep[e] = max(1, (c + EPOCH - 1) // EPOCH)
        with contextlib.ExitStack() as st:
            sems = {e: [st.enter_context(nc.semaphore("s_%s%d" % (e, i))) for i in range(nep[e])]
                    for e in ENGS if e != "sp"}
            dsems = [st.enter_context(nc.semaphore("d%d" % i)) for i in range(N_DMA_SEMS)]
            block = st.enter_context(nc.Block())
            engobj = {"pe": block.tensor, "act": block.scalar, "dve": block.vector,
                      "pool": block.gpsimd, "sp": block.sync}

            def mk(e):
                def body(eng):
                    for ins in self.q[e]:
                        for d in ins.deps:
                            if d.is_dma:
                                eng.wait_ge(dsems[d.slot], d.sval)
                            else:
                                eng.wait_ge(sems[d.eng][d.cnt // EPOCH], d.cnt % EPOCH + 1)
                        r = ins.fn(eng)
                        if ins.is_dma:
                            r.then_inc(dsems[ins.slot], 16)
                        elif ins.signal:
                            r.then_inc(sems[ins.eng][ins.cnt // EPOCH], 1)
                    if e == "sp":
                        last = {}
                        for d in self.dmas:
                            last[d.slot] = d
                        for k, d in last.items():
                            eng.wait_ge(dsems[k], d.sval)
                return body

            for e in ENGS:
                engobj[e](mk(e))


class _Stop(Exception):
    pass


def build_program(S, depth=4, stage=99):
    def chk(n):
        if n >= stage:
            raise _Stop()
    NT = S // TT
    NB = S // 128
    HALF = S // 2
    HNT = NT // 2
    KCH = min(2048, HALF)
    NKQ = S // KCH
    KCB = KCH // 128
    nA = (depth + 1) // 2
    nB = depth // 2

    nc = bass.Bass("TRN2", target_bir_lowering=False)

    def din(name, shape, dt=F32):
        return nc.dram_tensor(name, list(shape), dt, kind="ExternalInput").ap()

    def dscr(name, shape, dt):
        return nc.dram_tensor(name, list(shape), dt, kind="Internal").ap()

    x_in = din("x", [S, D])
    mem_in = din("mem", [2, NMEM, D])
    flags_in = din("flags", [128, 2])
    ropeA_c = din("ropeA_c", [64, S])
    ropeA_s = din("ropeA_s", [64, S])
    ropeB_c = din("ropeB_c", [96, S])
    ropeB_s = din("ropeB_s", [96, S])
    W = {}
    W["ffn1_norm"] = din("ffn1_norm", [4, D])
    W["ffn1_w_gu"] = din("ffn1_w_gu", [4, D, 2 * FF])
    W["ffn1_w_down"] = din("ffn1_w_down", [4, FF, D])
    W["mix_norm"] = din("mix_norm", [4, D])
    W["mem_norm"] = din("mem_norm", [4, D])
    W["w_mem_kv"] = din("w_mem_kv", [4, D, 512])
    W["a_w_in"] = din("a_w_in", [2, D, 1536])
    W["a_sink"] = din("a_sink", [2, 12])
    W["b_w_in"] = din("b_w_in", [2, D, 928])
    W["b_q_norm"] = din("b_q_norm", [2, 384])
    W["b_w_q_up"] = din("b_w_q_up", [2, 384, 1152])
    W["b_kv_norm"] = din("b_kv_norm", [2, 256])
    W["b_w_kv_up"] = din("b_w_kv_up", [2, 256, 1536])
    W["w_o"] = din("w_o", [4, D, D])
    W["ffn2_norm"] = din("ffn2_norm", [4, D])
    W["ffn2_w_gu"] = din("ffn2_w_gu", [4, D, 2 * FF])
    W["ffn2_w_down"] = din("ffn2_w_down", [4, FF, D])
    W["final_norm"] = din("final_norm", [D])
    y_out = nc.dram_tensor("y", [S, D], F32, kind="ExternalOutput").ap()

    wgu_s = dscr("wgu_s", [8, 11, 128, 4096], BF16)
    wd_s = dscr("wd_s", [8, 8, 128, FC * 128], BF16)
    wpa_s = dscr("wpa_s", [2, 10, 128, 2048], BF16)
    wpb_s = dscr("wpb_s", [2, 4, 128, 2048], BF16)
    wqu_s = dscr("wqu_s", [2, 4, 128, 2048], BF16)
    wku_s = dscr("wku_s", [2, 128, 2048], BF16)
    wvu_s = dscr("wvu_s", [2, 128, 2048], BF16)
    wo_s = dscr("wo_s", [4, 8, 64, 2048], BF16)
    wmem_s = dscr("wmem_s", [4, 128, 4096], BF16)
    KT_s = dscr("KT_s", [12, 96, S], BF16)
    V_s = dscr("V_s", [12, 128, NB * 64], BF16)
    Q_s = dscr("Q_s", [16, 96, S], BF16)
    xs = dscr("xs", [NT, 128, 4096], F32)

    S_ = Sched(nc)
    _cap = {"on": False, "q": None}

    def op(*a, **k):
        if _cap["on"]:
            _cap["q"].append((a, k))
            return None
        return S_.op(*a, **k)

    with contextlib.ExitStack() as st:
        sb_bytes = [0]

        def sb(name, shape, dt):
            n = 1
            for d_ in shape[1:]:
                n *= d_
            sb_bytes[0] += n * (4 if dt == F32 else 2)
            return st.enter_context(nc.sbuf_tensor("sb_" + name, list(shape), dt))

        def pst(name, shape, dt):
            return st.enter_context(nc.psum_tensor("pp_" + name, list(shape), dt))

        xT = sb("xT", [128, DC, TT], F32)
        xT_t = TL("xT", DC)
        hT = sb("hT", [128, DC, TT], BF16)
        hT_t = TL("hT", DC)
        actT = sb("actT", [128, FC * TT], BF16)
        actT_t = TL("actT", FC)
        actT3 = actT[:, :].rearrange("p (f t) -> p f t", f=FC)
        xin = actT[:, 0:8192].bitcast(F32).rearrange("p (b d) -> p b d", b=4)
        scrF = actT[:, 0:FC * TT].bitcast(F32)
        gu = [sb("gu%d" % i, [128, 4096], BF16) for i in range(2)]
        gu_t = TL("gu", 2)
        wd = [sb("wd%d" % i, [128, FC * 128], BF16) for i in range(3)]
        wd_t = TL("wd", 3)
        wr = [sb("wr%d" % i, [128, 2048], BF16) for i in range(3)]
        wr_t = TL("wr", 3)
        tmpf = [sb("tmpf%d" % i, [128, TT], F32) for i in range(4)]
        tmpf_t = TL("tmpf", 4)
        sq = [sb("sq%d" % i, [128, TT], BF16) for i in range(2)]
        sq_t = TL("sq", 2)
        rstd = sb("rstd", [128, TT], F32)
        rstd_t = T("rstd")
        headbuf = sb("headbuf", [128, 16, TT], BF16)
        hb_t = TL("hb", 16)
        OT = sb("OT", [128, 16, TT], BF16)
        OT_t = TL("OT", 16)
        kbuf = [sb("kbuf%d" % i, [128, 2048], BF16) for i in range(2)]
        kbuf_t = TL("kbuf", 2)
        vbuf = [sb("vbuf%d" % i, [128, 16, 128], BF16) for i in range(2)]
        kwin = [sb("kwin%d" % i, [128, 768], BF16) for i in range(2)]
        kwin_t = TL("kwin", 2)
        wob = [sb("wob%d" % i, [128, 2048], BF16) for i in range(2)]
        wob_t = TL("wob", 2)
        vbuf_t = TL("vbuf", 2)
        pt = [sb("pt%d" % i, [128, TT], BF16) for i in range(4)]
        pt_t = TL("pt", 4)
        masks = sb("masks", [128, 8, TT], BF16)
        masks_t = T("masks")
        vstage = sb("vstage", [128, 12, 4, 64], BF16)
        vst_t = T("vstage")
        krope = sb("krope", [32, TT], BF16)
        krope_t = T("krope")
        gA = sb("gA", [128, 128], F32)
        gB = sb("gB", [128, 128], F32)
        gT = sb("gT", [128, 256], F32)
        g_t = T("gains")
        ident = sb("ident", [128, 128], F32)
        ident_t = T("ident")
        ones_bf = sb("ones_bf", [128, 128], BF16)
        sel_f = sb("sel_f", [128, 64], F32)
        ones_t = T("ones")
        denrow = sb("denrow", [128, TT], F32)
        denrow_t = T("denrow")
        lnb = [sb("lnb%d" % i, [64, TT], F32) for i in range(2)]
        lnb_t = TL("lnb", 2)
        esink = sb("esink", [64, 24], F32)
        esink_t = T("esink")
        flags = sb("flags", [128, 2], F32)
        flags_t = T("flags")
        rc = sb("rc", [96, TT], F32)
        rs = sb("rs", [96, TT], F32)
        rc32 = sb("rc32", [32, TT], F32)
        rs32 = sb("rs32", [32, TT], F32)
        rope_t = T("rope")
        mk = sb("mk", [128, 2, 4, NMEM], BF16)
        mv = sb("mv", [128, 2, 2, 4, 128], BF16)
        mkv_t = T("mkv")
        ps = [pst("ps%d" % i, [128, TT], F32) for i in range(8)]
        ps_t = TL("ps", 8)

        t_wffn = [DT("wffn") for _ in range(8)]
        t_wpa = [DT("wpa") for _ in range(2)]
        t_wpb = [DT("wpb") for _ in range(2)]
        t_wo = [DT("wo") for _ in range(4)]
        t_wmem = [DT("wmem") for _ in range(4)]
        t_xs = TL("xs", NT)
        t_kv = [DT("kvs") for _ in range(NT)]
        t_q = TL("qs", NT)

        _dbg_try = True
        op("pool", lambda e: e.memset(ident[:, :], 0.0), writes=[ident_t])
        op("pool", lambda e: e.affine_select(out=ident[:, :], in_=ident[:, :], pattern=[[-1, 128]],
                                             compare_op=ALU.not_equal, fill=1.0, base=0,
                                             channel_multiplier=1), reads=[ident_t], writes=[ident_t])
        op("pool", lambda e: e.memset(ones_bf[:, :], 1.0), writes=[ones_t])
        op("pool", lambda e: e.memset(sel_f[:, :], 0.0), writes=[ones_t])
        op("pool", lambda e: e.memset(sel_f[64:65, :], 1.0), writes=[ones_t])
        op("pool", lambda e: e.memset(denrow[:, :], 0.0), writes=[denrow_t])
        op("pool", lambda e: e.memset(mv[:, :, :, :, 64:128], 1.0), writes=[mkv_t])
        op("pool", lambda e: e.memset(mk[:, :, :, :], 0.0), writes=[mkv_t])
        for i_ in range(2):
            op("pool", lambda e, i_=i_: e.memset(vbuf[i_][:, :, 64:128], 1.0), writes=[vbuf_t[i_]])
            op("pool", lambda e, i_=i_: e.memset(kwin[i_][:, :], 0.0), writes=[kwin_t[i_]])
            op("pool", lambda e, i_=i_: e.memset(wob[i_][:, :], 0.0), writes=[wob_t[i_]])
        op("pool", lambda e: e.memset(headbuf[:, :, :], 0.0), writes=hb_t)
        op("pool", lambda e: e.memset(OT[:, :, :], 0.0), writes=OT_t)
        op("pool", lambda e: e.memset(wr[0][:, :], 0.0), writes=[wr_t[0]])
        op("pool", lambda e: e.memset(gA[:, :], 0.0), writes=[g_t])
        op("pool", lambda e: e.memset(gB[:, :], 0.0), writes=[g_t])
        op("sp", lambda e: e.dma_start(out=flags[:, :], in_=flags_in[:, :]), writes=[flags_t], dma=True)
        op("pool", lambda e: e.memset(masks[:, :, :], 1.0), writes=[masks_t])
        for jj in range(6):
            op("pool", lambda e, jj=jj: e.affine_select(
                out=masks[:, jj, :], in_=masks[:, jj, :], pattern=[[-1, TT]], compare_op=ALU.is_ge,
                fill=0.0, base=128 + (jj - 1) * 128, channel_multiplier=1), reads=[masks_t], writes=[masks_t])
            op("pool", lambda e, jj=jj: e.affine_select(
                out=masks[:, jj, :], in_=masks[:, jj, :], pattern=[[1, TT]], compare_op=ALU.is_ge,
                fill=0.0, base=128 - (jj - 1) * 128, channel_multiplier=-1), reads=[masks_t], writes=[masks_t])
        op("pool", lambda e: e.tensor_scalar(masks[:, 6, :], masks[:, 0, :], flags[:, 0:1], None, ALU.mult),
           reads=[masks_t, flags_t], writes=[masks_t])
        op("pool", lambda e: e.tensor_scalar(masks[:, 7, :], masks[:, 5, :], flags[:, 0:1], None, ALU.mult),
           reads=[masks_t, flags_t], writes=[masks_t])
        for gi, nm in enumerate(["ffn1_norm", "mix_norm", "mem_norm", "ffn2_norm"]):
            op("sp", lambda e, gi=gi, nm=nm: e.dma_start(
                out=gA[gi * 32:(gi + 1) * 32, :], in_=W[nm][:, :].rearrange("l (c p) -> (l c) p", p=128)),
               writes=[g_t], dma=True)
        op("sp", lambda e: e.dma_start(out=gB[0:8, :], in_=W["final_norm"].rearrange("(c p) -> c p", p=128)),
           writes=[g_t], dma=True)
        op("sp", lambda e: e.dma_start(out=gB[8:14, :], in_=W["b_q_norm"][:, :].rearrange("l (c p) -> (l c) p", p=128)),
           writes=[g_t], dma=True)
        op("sp", lambda e: e.dma_start(out=gB[14:18, :], in_=W["b_kv_norm"][:, :].rearrange("l (c p) -> (l c) p", p=128)),
           writes=[g_t], dma=True)
        op("pe", lambda e: e.transpose(ps[6][:, 0:128], gA[:, :], ident[:, :]), reads=[g_t, ident_t], writes=[ps_t[6]])
        op("pe", lambda e: e.transpose(ps[6][:, 128:256], gB[:, :], ident[:, :]), reads=[g_t, ident_t], writes=[ps_t[6]])
        gT_t = T("gT")
        op("act", lambda e: e.activation(out=gT[:, :], in_=ps[6][:, 0:256], func=AF.Copy), reads=[ps_t[6]], writes=[gT_t])

        def g_ffn1(l, c): return gT[:, l * 8 + c: l * 8 + c + 1]
        def g_mix(l, c): return gT[:, 32 + l * 8 + c: 32 + l * 8 + c + 1]
        def g_mem(l, c): return gT[:, 64 + l * 8 + c: 64 + l * 8 + c + 1]
        def g_ffn2(l, c): return gT[:, 96 + l * 8 + c: 96 + l * 8 + c + 1]
        def g_fin(c): return gT[:, 128 + c: 128 + c + 1]
        def g_bq(j, c): return gT[:, 136 + j * 3 + c: 136 + j * 3 + c + 1]
        def g_bkv(j, c): return gT[:, 142 + j * 2 + c: 142 + j * 2 + c + 1]

        op("sp", lambda e: e.dma_start(out=esink[:, :], in_=W["a_sink"][:, :].rearrange("a b -> (a b)").partition_broadcast(64)),
           writes=[esink_t], dma=True)
        op("act", lambda e: e.activation(out=esink[:, :], in_=esink[:, :], func=AF.Exp), reads=[esink_t], writes=[esink_t])

        def cvt(out_ap, in_ap, tw, extra_reads=()):
            op("pool", lambda e: e.dma_start(out=out_ap, in_=in_ap), reads=list(extra_reads), writes=tw.w(), dma=True)

        def cvt_ffn(k):
            l = k // 2
            wg = W["ffn1_w_gu" if k % 2 == 0 else "ffn2_w_gu"][l]
            wdn = W["ffn1_w_down" if k % 2 == 0 else "ffn2_w_down"][l]
            for j in range(11):
                dst = wgu_s[k, j].rearrange("p (c f) -> p c f", c=8)
                cvt(dst[:, :, 0:256], wg[:, j * 256:(j + 1) * 256].rearrange("(c p) f -> p c f", p=128), t_wffn[k])
                cvt(dst[:, :, 256:512], wg[:, FF + j * 256:FF + (j + 1) * 256].rearrange("(c p) f -> p c f", p=128), t_wffn[k])
            for c in range(8):
                cvt(wd_s[k, c].rearrange("p (f d) -> p f d", f=FC),
                    wdn[:, c * 128:(c + 1) * 128].rearrange("(f p) d -> p f d", p=128), t_wffn[k])

        def cvt_cols(dst3, src2d, c0, n, d0=0):
            return (dst3[:, :, d0:d0 + n], src2d[:, c0:c0 + n].rearrange("(c p) f -> p c f", p=128))

        def cvt_swapped(dst3, src2d, c0, nheads, hd, tw, d0=0):
            h2 = hd // 2
            for c in range(dst3.shape[1]):
                d4 = dst3[:, c, d0:d0 + nheads * hd].rearrange("p (h e) -> p h e", e=hd)
                s4 = src2d[c * 128:(c + 1) * 128, c0:c0 + nheads * hd].rearrange("p (h e) -> p h e", e=hd)
                cvt(d4[:, :, 0:h2], s4[:, :, h2:hd], tw)
                cvt(d4[:, :, h2:hd], s4[:, :, 0:h2], tw)

        def cvt_layerA(j):
            w = W["a_w_in"][j]
            tw = t_wpa[j]
            def dst(pc): return wpa_s[j, pc].rearrange("p (c f) -> p c f", c=8)
            for hg in range(3):
                cvt(*cvt_cols(dst(hg), w, hg * 256, 256), tw)
                cvt_swapped(dst(3 + hg), w, hg * 256, 4, 64, tw)
            cvt(*cvt_cols(dst(6), w, 768, 256), tw)
            cvt_swapped(dst(7), w, 768, 4, 64, tw)
            cvt(*cvt_cols(dst(8), w, 1024, 256), tw)
            cvt(*cvt_cols(dst(9), w, 1280, 256), tw)


        def cvt_layerB(j):
            w = W["b_w_in"][j]
            tw = t_wpb[j]
            def dst(pc): return wpb_s[j, pc].rearrange("p (c f) -> p c f", c=8)
            cvt(*cvt_cols(dst(0), w, 0, 256), tw)
            cvt(*cvt_cols(dst(1), w, 256, 128), tw)
            cvt(*cvt_cols(dst(1), w, 640, 32, d0=128), tw)
            cvt_swapped(dst(1), w, 640, 1, 32, tw, d0=160)
            cvt(*cvt_cols(dst(2), w, 384, 256), tw)
            cvt(*cvt_cols(dst(3), w, 672, 256), tw)
            wq = W["b_w_q_up"][j]
            for pc in range(2):
                d3 = wqu_s[j, pc][:, 0:1728].rearrange("p (c f) -> p c f", c=3)
                cvt(d3, wq[:, pc * 576:(pc + 1) * 576].rearrange("(c p) f -> p c f", p=128), tw)
                d3s = wqu_s[j, 2 + pc][:, 0:1728].rearrange("p (c f) -> p c f", c=3)
                zw = tw.w()
                op("pool", lambda e, dz=wqu_s[j, 2 + pc][:, 0:1728]: e.dma_start(out=dz, in_=zero_sb[:, 0:1728]),
                   reads=[zeros_t], writes=zw, dma=True)
                for c in range(3):
                    d4 = d3s[:, c, :].rearrange("p (h e) -> p h e", e=96)
                    s4 = wq[c * 128:(c + 1) * 128, pc * 576:(pc + 1) * 576].rearrange("p (h e) -> p h e", e=96)
                    cvt(d4[:, :, 64:80], s4[:, :, 80:96], tw, extra_reads=zw[-1:])
                    cvt(d4[:, :, 80:96], s4[:, :, 64:80], tw, extra_reads=zw[-1:])
            wkv = W["b_w_kv_up"][j]
            dk = wku_s[j][:, 0:1536].rearrange("p (c h e) -> p c h e", c=2, h=12)
            dv = wvu_s[j][:, 0:1536].rearrange("p (c h e) -> p c h e", c=2, h=12)
            for c in range(2):
                s4 = wkv[c * 128:(c + 1) * 128, :].rearrange("p (h e) -> p h e", e=128)
                cvt(dk[:, c, :, :], s4[:, :, 0:64], tw)
                cvt(dv[:, c, :, :], s4[:, :, 64:128], tw)

        def cvt_wo(l):
            for c in range(8):
                cvt(wo_s[l, c].rearrange("r (s d) -> r s d", s=16),
                    W["w_o"][l][:, c * 128:(c + 1) * 128].rearrange("(s r) d -> r s d", r=64), t_wo[l])

        def cvt_mem(l):
            cvt(wmem_s[l].rearrange("p (c f) -> p c f", c=8),
                W["w_mem_kv"][l].rearrange("(c p) f -> p c f", p=128), t_wmem[l])

        zero_sb = wr[0]
        zeros_t = wr_t[0]

        convq = {}
        for l in range(depth):
            if l > 0:
                _cap["on"] = True
                _cap["q"] = []
            cvt_ffn(2 * l)
            if l % 2 == 0:
                cvt_layerA(l // 2)
            else:
                cvt_layerB(l // 2)
            cvt_mem(l)
            cvt_wo(l)
            cvt_ffn(2 * l + 1)
            if l > 0:
                convq[l] = _cap["q"]
                _cap["on"] = False

        def conv_emit(l, n):
            q = convq.get(l)
            while q and n > 0:
                a_, k_ = q.pop(0)
                S_.op(*a_, **k_)
                n -= 1

        rot = {"wob": 0, "sq": 0, "tmp": 0, "gu": 0, "wd": 0, "wr": 0, "pj": 0, "pt": 0, "st": 0, "kb": 0, "ob": 0, "ln": 0, "tr": 0}

        def nxt(key, n):
            v = rot[key]
            rot[key] = (v + 1) % n
            return v

        def rms_stats(src_fn, src_ts, nchunk, width, inv_n):
            for c in range(nchunk):
                k = nxt("sq", 2)
                if c % 2 == 0:
                    op("act", lambda e, c=c, k=k: e.activation(out=sq[k][:, 0:width], in_=src_fn(c), func=AF.Square),
                       reads=[src_ts[c]], writes=[sq_t[k]])
                else:
                    op("dve", lambda e, c=c, k=k: e.tensor_tensor(sq[k][:, 0:width], src_fn(c), src_fn(c), ALU.mult),
                       reads=[src_ts[c]], writes=[sq_t[k]])
                op("pe", lambda e, c=c, k=k: e.matmul(ps[6][:, 0:width], ones_bf[:, :], sq[k][:, 0:width],
                                                      start=(c == 0), stop=(c == nchunk - 1)),
                   reads=[sq_t[k], ones_t], writes=[ps_t[6]])
            op("dve", lambda e: e.tensor_scalar(rstd[:, 0:width], ps[6][:, 0:width], inv_n, EPS, ALU.mult, ALU.add),
               reads=[ps_t[6]], writes=[rstd_t])
            op("act", lambda e: e.activation(out=rstd[:, 0:width], in_=rstd[:, 0:width], func=AF.Ln),
               reads=[rstd_t], writes=[rstd_t])
            op("act", lambda e: e.activation(out=rstd[:, 0:width], in_=rstd[:, 0:width], func=AF.Exp, scale=-0.5),
               reads=[rstd_t], writes=[rstd_t])

        def rms_apply(src_fn, src_ts, dst_fn, dst_ts, nchunk, gfn):
            for c in range(nchunk):
                op("dve", lambda e, c=c: e.scalar_tensor_tensor(dst_fn(c), src_fn(c), gfn(c), rstd_of(src_fn(c)),
                                                               ALU.mult, ALU.mult),
                   reads=[src_ts[c], rstd_t, gT_t], writes=[dst_ts[c]])

        def rstd_of(ap):
            return rstd[:, 0:ap.shape[-1]]

        def ffn(k, gfn):
            rms_stats(lambda c: xT[:, c, :], xT_t, DC, TT, 1.0 / D)
            rms_apply(lambda c: xT[:, c, :], xT_t, lambda c: hT[:, c, :], hT_t, DC, gfn)
            for j in range(11):
                r = nxt("gu", 2)
                op("sp", lambda e, r=r, j=j: e.dma_start(out=gu[r][:, :], in_=wgu_s[k, j]),
                   reads=[t_wffn[k]], writes=[gu_t[r]], dma=True)
                g3 = gu[r][:, :].rearrange("p (c f) -> p c f", c=8)
                for jj in range(2):
                    f = 2 * j + jj
                    bg = (f % 2) * 2
                    bu = bg + 1
                    for c in range(DC):
                        op("pe", lambda e, c=c, g3=g3, jj=jj, bg=bg: e.matmul(
                            ps[bg][:, :], g3[:, c, jj * 128:(jj + 1) * 128], hT[:, c, :], start=(c == 0), stop=(c == DC - 1)),
                           reads=[gu_t[r], hT_t[c]], writes=[ps_t[bg]])
                    for c in range(DC):
                        op("pe", lambda e, c=c, g3=g3, jj=jj, bu=bu: e.matmul(
                            ps[bu][:, :], g3[:, c, 256 + jj * 128:256 + (jj + 1) * 128], hT[:, c, :], start=(c == 0), stop=(c == DC - 1)),
                           reads=[gu_t[r], hT_t[c]], writes=[ps_t[bu]])
                    tk = nxt("tmp", 4)
                    op("act", lambda e, tk=tk, bg=bg: e.activation(out=tmpf[tk][:, :], in_=ps[bg][:, :], func=AF.Silu),
                       reads=[ps_t[bg]], writes=[tmpf_t[tk]])
                    op("dve", lambda e, tk=tk, bu=bu, f=f: e.tensor_tensor(actT3[:, f, :], tmpf[tk][:, :], ps[bu][:, :], ALU.mult),
                       reads=[tmpf_t[tk], ps_t[bu]], writes=[actT_t[f]])
            for c in range(DC):
                r = nxt("wd", 3)
                op("sp", lambda e, r=r, c=c: e.dma_start(out=wd[r][:, :], in_=wd_s[k, c]),
                   reads=[t_wffn[k]], writes=[wd_t[r]], dma=True)
                w3 = wd[r][:, :].rearrange("p (f d) -> p f d", f=FC)
                b = 4 + (c % 2)
                for f in range(FC):
                    op("pe", lambda e, f=f, w3=w3, b=b: e.matmul(ps[b][:, :], w3[:, f, :], actT3[:, f, :],
                                                                 start=(f == 0), stop=(f == FC - 1)),
                       reads=[wd_t[r], actT_t[f]], writes=[ps_t[b]])
                op("dve", lambda e, c=c, b=b: e.scalar_tensor_tensor(xT[:, c, :], ps[b][:, :], 0.5, xT[:, c, :], ALU.mult, ALU.add),
                   reads=[ps_t[b], xT_t[c]], writes=[xT_t[c]])

        def load_wr(src_ap, tw, rows=128, n=2048):
            r = nxt("wr", 3)
            op("sp", lambda e: e.dma_start(out=wr[r][0:rows, 0:n], in_=src_ap), reads=[tw], writes=[wr_t[r]], dma=True)
            return r

        def pj_bank():
            return 4 + nxt("pj", 4)

        def proj(bank, rows, lhs_fn, rhs_fn, nk, reads, width=TT):
            for c in range(nk):
                op("pe", lambda e, c=c: e.matmul(ps[bank][0:rows, 0:width], lhs_fn(c), rhs_fn(c),
                                                 start=(c == 0), stop=(c == nk - 1)),
                   reads=reads(c), writes=[ps_t[bank]])

        def rope_combine(bA, bB, rows, ctab, stab, dst_ap, dst_t):
            t1 = nxt("tmp", 4)
            t2 = nxt("tmp", 4)
            op("dve", lambda e: e.tensor_tensor(tmpf[t1][0:rows, :], ps[bA][0:rows, :], ctab, ALU.mult),
               reads=[ps_t[bA], rope_t], writes=[tmpf_t[t1]])
            op("dve", lambda e: e.tensor_tensor(tmpf[t2][0:rows, :], ps[bB][0:rows, :], stab, ALU.mult),
               reads=[ps_t[bB], rope_t], writes=[tmpf_t[t2]])
            op("pool", lambda e: e.tensor_tensor(dst_ap, tmpf[t1][0:rows, :], tmpf[t2][0:rows, :], ALU.add),
               reads=[tmpf_t[t1], tmpf_t[t2]], writes=[dst_t])

        def load_x_tile(i):
            op("sp", lambda e: e.dma_start(out=xT[:, :, :], in_=xs[i].rearrange("p (c t) -> p c t", c=DC)),
               reads=[t_xs[i]], writes=xT_t, dma=True)

        def store_x_tile(i):
            op("pool", lambda e: e.dma_start(out=xs[i].rearrange("p (c t) -> p c t", c=DC), in_=xT[:, :, :]),
               reads=xT_t, writes=[t_xs[i]], dma=True)

        def load_x_input(i):
            op("sp", lambda e: e.dma_start(out=xin[:, :, :], in_=x_in[i * TT:(i + 1) * TT, :].rearrange("(b p) d -> p b d", p=128)),
               writes=actT_t[0:16], dma=True)
            for c in range(DC):
                b = 6 + nxt("tr", 2)
                for blk in range(4):
                    op("pe", lambda e, c=c, blk=blk, b=b: e.transpose(ps[b][:, blk * 128:(blk + 1) * 128],
                                                                      xin[:, blk, c * 128:(c + 1) * 128], ident[:, :]),
                       reads=actT_t[0:16] + [ident_t], writes=[ps_t[b]])
                op("act", lambda e, c=c, b=b: e.activation(out=xT[:, c, :], in_=ps[b][:, :], func=AF.Copy),
                   reads=[ps_t[b]], writes=[xT_t[c]])

        def final_out(i):
            rms_stats(lambda c: xT[:, c, :], xT_t, DC, TT, 1.0 / D)
            rms_apply(lambda c: xT[:, c, :], xT_t, lambda c: xT[:, c, :], xT_t, DC, g_fin)
            for blk in range(4):
                for cg in range(2):
                    b = 6 + nxt("tr", 2)
                    for cc in range(4):
                        c = cg * 4 + cc
                        op("pe", lambda e, c=c, cc=cc, blk=blk, b=b: e.transpose(
                            ps[b][:, cc * 128:(cc + 1) * 128], xT[:, c, blk * 128:(blk + 1) * 128], ident[:, :]),
                           reads=[xT_t[c], ident_t], writes=[ps_t[b]])
                    op("act", lambda e, blk=blk, cg=cg, b=b: e.activation(out=xin[:, blk, cg * 512:(cg + 1) * 512], in_=ps[b][:, :], func=AF.Copy),
                       reads=[ps_t[b]], writes=actT_t[0:16])
            op("pool", lambda e: e.dma_start(out=y_out[i * TT:(i + 1) * TT, :].rearrange("(b p) d -> p b d", p=128), in_=xin[:, :, :]),
               reads=actT_t[0:16], dma=True)

        def load_rope(i, layerA):
            cs = slice(i * TT, (i + 1) * TT)
            if layerA:
                op("sp", lambda e: e.dma_start(out=rc[0:64, :], in_=ropeA_c[:, cs]), writes=[rope_t], dma=True)
                op("sp", lambda e: e.dma_start(out=rs[0:64, :], in_=ropeA_s[:, cs]), writes=[rope_t], dma=True)
            else:
                op("sp", lambda e: e.dma_start(out=rc[:, :], in_=ropeB_c[:, cs]), writes=[rope_t], dma=True)
                op("sp", lambda e: e.dma_start(out=rs[:, :], in_=ropeB_s[:, cs]), writes=[rope_t], dma=True)
                op("sp", lambda e: e.dma_start(out=rc32[:, :], in_=ropeB_c[64:96, cs]), writes=[rope_t], dma=True)
                op("sp", lambda e: e.dma_start(out=rs32[:, :], in_=ropeB_s[64:96, cs]), writes=[rope_t], dma=True)

        def inproj_A(j, i):
            cs = slice(i * TT, (i + 1) * TT)
            tw = t_wpa[j]
            load_rope(i, True)
            for grp in range(4):
                rN = load_wr(wpa_s[j, grp if grp < 3 else 6], tw)
                rS = load_wr(wpa_s[j, 3 + grp if grp < 3 else 7], tw)
                wN = wr[rN][:, :].rearrange("p (c f) -> p c f", c=8)
                wS = wr[rS][:, :].rearrange("p (c f) -> p c f", c=8)
                for hh in range(4):
                    bA = pj_bank()
                    bB = pj_bank()
                    proj(bA, 64, lambda c, hh=hh, wN=wN: wN[:, c, hh * 64:(hh + 1) * 64], lambda c: hT[:, c, :], DC,
                         lambda c, rN=rN: [wr_t[rN], hT_t[c]])
                    proj(bB, 64, lambda c, hh=hh, wS=wS: wS[:, c, hh * 64:(hh + 1) * 64], lambda c: hT[:, c, :], DC,
                         lambda c, rS=rS: [wr_t[rS], hT_t[c]])
                    if grp < 3:
                        h = grp * 4 + hh
                        rope_combine(bA, bB, 64, rc[0:64, :], rs[0:64, :], headbuf[0:64, h, :], hb_t[h])
                    else:
                        rope_combine(bA, bB, 64, rc[0:64, :], rs[0:64, :], OT[0:64, hh, :], OT_t[hh])
            rV = load_wr(wpa_s[j, 8], tw)
            wV = wr[rV][:, :].rearrange("p (c f) -> p c f", c=8)
            for blk in range(4):
                b = pj_bank()
                proj(b, 128, lambda c, blk=blk: hT[:, c, blk * 128:(blk + 1) * 128], lambda c, wV=wV: wV[:, c, 0:256], DC,
                     lambda c, rV=rV: [wr_t[rV], hT_t[c]], width=256)
                op("act", lambda e, b=b, blk=blk: e.activation(out=vstage[:, 0:4, blk, 0:64],
                                                               in_=ps[b][:, 0:256].rearrange("p (g e) -> p g e", g=4), func=AF.Copy),
                   reads=[ps_t[b]], writes=[vst_t])
            rQ = load_wr(wpa_s[j, 9], tw)
            wQ = wr[rQ][:, :].rearrange("p (c f) -> p c f", c=8)
            for m in range(4):
                b = pj_bank()
                proj(b, 64, lambda c, m=m, wQ=wQ: wQ[:, c, m * 64:(m + 1) * 64], lambda c: hT[:, c, :], DC,
                     lambda c, rQ=rQ: [wr_t[rQ], hT_t[c]])
                op("act", lambda e, b=b, m=m: e.activation(out=headbuf[0:64, 12 + m, :], in_=ps[b][0:64, :], func=AF.Copy),
                   reads=[ps_t[b]], writes=[hb_t[12 + m]])
            op("pool", lambda e: e.dma_start(out=Q_s[0:16, 0:64, cs].rearrange("h r t -> r h t"), in_=headbuf[0:64, 0:16, :]),
               reads=hb_t, writes=[t_q[i]], dma=True)
            op("pool", lambda e: e.dma_start(out=KT_s[0:4, 0:64, cs].rearrange("h r t -> r h t"), in_=OT[0:64, 0:4, :]),
               reads=OT_t[0:4], writes=t_kv[i].w(), dma=True)
            op("pool", lambda e: e.dma_start(
                out=V_s[0:4, :, i * 256:(i + 1) * 256].rearrange("h p (b e) -> p h b e", b=4),
                in_=vstage[:, 0:4, :, :]), reads=[vst_t], writes=t_kv[i].w(), dma=True)

        def inproj_B(j, i):
            cs = slice(i * TT, (i + 1) * TT)
            tw = t_wpb[j]
            load_rope(i, False)
            cq = scrF[:, 0:1536].rearrange("p (c t) -> p c t", c=3)
            ckv = scrF[:, 1536:2560].rearrange("p (c t) -> p c t", c=2)
            A_ALL = actT_t
            r0 = load_wr(wpb_s[j, 0], tw)
            w0 = wr[r0][:, :].rearrange("p (c f) -> p c f", c=8)
            r1 = load_wr(wpb_s[j, 1], tw)
            w1 = wr[r1][:, :].rearrange("p (c f) -> p c f", c=8)
            for cc in range(3):
                b = pj_bank()
                if cc < 2:
                    proj(b, 128, lambda c, cc=cc: w0[:, c, cc * 128:(cc + 1) * 128], lambda c: hT[:, c, :], DC,
                         lambda c: [wr_t[r0], hT_t[c]])
                else:
                    proj(b, 128, lambda c: w1[:, c, 0:128], lambda c: hT[:, c, :], DC, lambda c: [wr_t[r1], hT_t[c]])
                op("act", lambda e, b=b, cc=cc: e.activation(out=cq[:, cc, :], in_=ps[b][:, :], func=AF.Copy),
                   reads=[ps_t[b]], writes=A_ALL)
            bA = pj_bank()
            bB = pj_bank()
            proj(bA, 32, lambda c: w1[:, c, 128:160], lambda c: hT[:, c, :], DC, lambda c: [wr_t[r1], hT_t[c]])
            proj(bB, 32, lambda c: w1[:, c, 160:192], lambda c: hT[:, c, :], DC, lambda c: [wr_t[r1], hT_t[c]])
            rope_combine(bA, bB, 32, rc32[:, :], rs32[:, :], krope[:, :], krope_t)
            r2 = load_wr(wpb_s[j, 2], tw)
            w2 = wr[r2][:, :].rearrange("p (c f) -> p c f", c=8)
            for cc in range(2):
                b = pj_bank()
                proj(b, 128, lambda c, cc=cc: w2[:, c, cc * 128:(cc + 1) * 128], lambda c: hT[:, c, :], DC,
                     lambda c: [wr_t[r2], hT_t[c]])
                op("act", lambda e, b=b, cc=cc: e.activation(out=ckv[:, cc, :], in_=ps[b][:, :], func=AF.Copy),
                   reads=[ps_t[b]], writes=A_ALL)
            r3 = load_wr(wpb_s[j, 3], tw)
            w3 = wr[r3][:, :].rearrange("p (c f) -> p c f", c=8)
            for m in range(4):
                b = pj_bank()
                proj(b, 64, lambda c, m=m: w3[:, c, m * 64:(m + 1) * 64], lambda c: hT[:, c, :], DC,
                     lambda c: [wr_t[r3], hT_t[c]])
                op("act", lambda e, b=b, m=m: e.activation(out=headbuf[0:64, 12 + m, :], in_=ps[b][0:64, :], func=AF.Copy),
                   reads=[ps_t[b]], writes=[hb_t[12 + m]])
            rms_stats(lambda c: cq[:, c, :], [A_ALL] * 3, 3, TT, 1.0 / 384)
            rms_apply(lambda c: cq[:, c, :], [A_ALL] * 3, lambda c: hT[:, c, :], hT_t[0:3], 3, lambda c: g_bq(j, c))
            rms_stats(lambda c: ckv[:, c, :], [A_ALL] * 2, 2, TT, 1.0 / 256)
            rms_apply(lambda c: ckv[:, c, :], [A_ALL] * 2, lambda c: hT[:, 4 + c, :], hT_t[4:6], 2, lambda c: g_bkv(j, c))
            for pc in range(2):
                rN = load_wr(wqu_s[j, pc], tw)
                rS = load_wr(wqu_s[j, 2 + pc], tw)
                wN = wr[rN][:, 0:1728].rearrange("p (c f) -> p c f", c=3)
                wS = wr[rS][:, 0:1728].rearrange("p (c f) -> p c f", c=3)
                for hh in range(6):
                    h = pc * 6 + hh
                    bA = pj_bank()
                    bB = pj_bank()
                    proj(bA, 96, lambda c, hh=hh, wN=wN: wN[:, c, hh * 96:(hh + 1) * 96], lambda c: hT[:, c, :], 3,
                         lambda c, rN=rN: [wr_t[rN], hT_t[c]])
                    proj(bB, 96, lambda c, hh=hh, wS=wS: wS[:, c, hh * 96:(hh + 1) * 96], lambda c: hT[:, c, :], 3,
                         lambda c, rS=rS: [wr_t[rS], hT_t[c]])
                    rope_combine(bA, bB, 96, rc[:, :], rs[:, :], headbuf[0:96, h, :], hb_t[h])
            rK = load_wr(wku_s[j], tw)
            wK = wr[rK][:, 0:1536].rearrange("p (c f) -> p c f", c=2)
            for h in range(12):
                b = pj_bank()
                proj(b, 64, lambda c, h=h: wK[:, c, h * 64:(h + 1) * 64], lambda c: hT[:, 4 + c, :], 2,
                     lambda c: [wr_t[rK], hT_t[4 + c]])
                op("act", lambda e, b=b, h=h: e.activation(out=OT[0:64, h, :], in_=ps[b][0:64, :], func=AF.Copy),
                   reads=[ps_t[b]], writes=[OT_t[h]])
            rV = load_wr(wvu_s[j], tw)
            wV = wr[rV][:, 0:1536].rearrange("p (c f) -> p c f", c=2)
            for blk in range(4):
                b = pj_bank()
                proj(b, 128, lambda c, blk=blk: hT[:, 4 + c, blk * 128:(blk + 1) * 128], lambda c: wV[:, c, 0:512], 2,
                     lambda c: [wr_t[rV], hT_t[4 + c]])
                op("act", lambda e, b=b, blk=blk: e.activation(out=vstage[:, 0:8, blk, 0:64],
                                                               in_=ps[b][:, 0:512].rearrange("p (g e) -> p g e", g=8), func=AF.Copy),
                   reads=[ps_t[b]], writes=[vst_t])
                b = pj_bank()
                proj(b, 128, lambda c, blk=blk: hT[:, 4 + c, blk * 128:(blk + 1) * 128], lambda c: wV[:, c, 512:768], 2,
                     lambda c: [wr_t[rV], hT_t[4 + c]], width=256)
                op("act", lambda e, b=b, blk=blk: e.activation(out=vstage[:, 8:12, blk, 0:64],
                                                               in_=ps[b][:, 0:256].rearrange("p (g e) -> p g e", g=4), func=AF.Copy),
                   reads=[ps_t[b]], writes=[vst_t])
            op("pool", lambda e: e.dma_start(out=Q_s[0:12, 0:96, cs].rearrange("h r t -> r h t"), in_=headbuf[0:96, 0:12, :]),
               reads=hb_t[0:12], writes=[t_q[i]], dma=True)
            op("pool", lambda e: e.dma_start(out=Q_s[12:16, 0:64, cs].rearrange("h r t -> r h t"), in_=headbuf[0:64, 12:16, :]),
               reads=hb_t[12:16], writes=[t_q[i]], dma=True)
            op("pool", lambda e: e.dma_start(out=KT_s[0:12, 0:64, cs].rearrange("h r t -> r h t"), in_=OT[0:64, 0:12, :]),
               reads=OT_t[0:12], writes=t_kv[i].w(), dma=True)
            for h in range(12):
                op("pool", lambda e, h=h: e.dma_start(out=KT_s[h, 64:96, cs], in_=krope[:, :]),
                   reads=[krope_t], writes=t_kv[i].w(), dma=True)
            op("pool", lambda e: e.dma_start(
                out=V_s[0:12, :, i * 256:(i + 1) * 256].rearrange("h p (b e) -> p h b e", b=4),
                in_=vstage[:, :, :, :]), reads=[vst_t], writes=t_kv[i].w(), dma=True)

        def mem_setup(l):
            r = nxt("gu", 2)
            op("sp", lambda e: e.dma_start(out=gu[r][:, :], in_=wmem_s[l]), reads=[t_wmem[l]], writes=[gu_t[r]], dma=True)
            wm = gu[r][:, :].rearrange("p (c f) -> p c f", c=8)
            for hf in range(2):
                op("sp", lambda e, hf=hf: e.dma_start(out=xin[:, 0:2, :], in_=mem_in[hf].rearrange("(b p) d -> p b d", p=128)),
                   writes=actT_t[0:16], dma=True)
                for c in range(DC):
                    b = 6 + nxt("tr", 2)
                    for blk in range(2):
                        op("pe", lambda e, c=c, blk=blk, b=b: e.transpose(ps[b][:, blk * 128:(blk + 1) * 128],
                                                                          xin[:, blk, c * 128:(c + 1) * 128], ident[:, :]),
                           reads=actT_t[0:16] + [ident_t], writes=[ps_t[b]])
                    op("act", lambda e, c=c, b=b: e.activation(out=xT[:, c, 0:NMEM], in_=ps[b][:, 0:NMEM], func=AF.Copy),
                       reads=[ps_t[b]], writes=[xT_t[c]])
                rms_stats(lambda c: xT[:, c, 0:NMEM], xT_t, DC, NMEM, 1.0 / D)
                rms_apply(lambda c: xT[:, c, 0:NMEM], xT_t, lambda c: hT[:, c, 0:NMEM], hT_t, DC, lambda c: g_mem(l, c))
                for m in range(4):
                    b = pj_bank()
                    proj(b, 64, lambda c, m=m: wm[:, c, m * 64:(m + 1) * 64], lambda c: hT[:, c, 0:NMEM], DC,
                         lambda c: [gu_t[r], hT_t[c]], width=NMEM)
                    op("act", lambda e, b=b, m=m, hf=hf: e.activation(out=mk[0:64, hf, m, :], in_=ps[b][0:64, 0:NMEM], func=AF.Copy),
                       reads=[ps_t[b]], writes=[mkv_t])
                for ch in range(2):
                    b = pj_bank()
                    proj(b, 128, lambda c, ch=ch: hT[:, c, ch * 128:(ch + 1) * 128], lambda c: wm[:, c, 256:512], DC,
                         lambda c: [gu_t[r], hT_t[c]], width=256)
                    op("act", lambda e, b=b, ch=ch, hf=hf: e.activation(out=mv[:, hf, ch, :, 0:64],
                                                                       in_=ps[b][:, 0:256].rearrange("p (g e) -> p g e", g=4), func=AF.Copy),
                       reads=[ps_t[b]], writes=[mkv_t])

        LOOK = 3
        pipe = {"fifo": [], "defer": []}

        def pipe_pop(limit):
            while pipe["fifo"]:
                npv = sum(1 for k, _ in pipe["fifo"] if k == "pv")
                if pipe["fifo"][0][0] == "fin" or npv > limit:
                    pipe["fifo"].pop(0)[1]()
                    continue
                break

        def pipe_tick():
            nd = []
            for cnt, fn in pipe["defer"]:
                if cnt <= 1:
                    fn()
                else:
                    nd.append((cnt - 1, fn))
            pipe["defer"] = nd

        def pipe_flush():
            pipe_pop(0)
            while pipe["fifo"]:
                pipe["fifo"].pop(0)[1]()
            for _, fn in pipe["defer"]:
                fn()
            pipe["defer"] = []

        def attn_unit(bO, first, last, klhs, klhs_reads, qrhs, q_t, scale, bias_ap, mask_idx, vlhs, v_reads):
            bS = nxt("st", 4)
            op("pe", lambda e: e.matmul(ps[bS][:, :], klhs, qrhs, start=True, stop=True),
               reads=klhs_reads + [q_t], writes=[ps_t[bS]])
            p = nxt("pt", 4)
            if bias_ap is None:
                op("act", lambda e: e.activation(out=pt[p][:, :], in_=ps[bS][:, :], func=AF.Exp, scale=scale),
                   reads=[ps_t[bS]], writes=[pt_t[p]])
            else:
                op("act", lambda e: e.activation(out=pt[p][:, :], in_=ps[bS][:, :], func=AF.Exp, scale=scale, bias=bias_ap),
                   reads=[ps_t[bS], flags_t], writes=[pt_t[p]])
            if mask_idx is not None:
                op("dve", lambda e: e.tensor_tensor(pt[p][:, :], pt[p][:, :], masks[:, mask_idx, :], ALU.mult),
                   reads=[pt_t[p], masks_t], writes=[pt_t[p]])
            pipe_tick()

            def pv():
                op("pe", lambda e: e.matmul(ps[bO][:, :], vlhs, pt[p][:, :], start=first, stop=last),
                   reads=v_reads + [pt_t[p]], writes=[ps_t[bO]])
            pipe["fifo"].append(("pv", pv))
            pipe_pop(LOOK)

        def attn_finish(bO, slot, sink_ap):
            def stage2():
                op("pe", lambda e: e.matmul(ps[6][0:64, :], sel_f[:, :], denrow[:, :], start=True, stop=True),
                   reads=[denrow_t, ones_t], writes=[ps_t[6]])
                k = nxt("ln", 2)
                if sink_ap is None:
                    op("act", lambda e: e.activation(out=lnb[k][:, :], in_=ps[6][0:64, :], func=AF.Ln),
                       reads=[ps_t[6]], writes=[lnb_t[k]])
                else:
                    op("act", lambda e: e.activation(out=lnb[k][:, :], in_=ps[6][0:64, :], func=AF.Ln, bias=sink_ap),
                       reads=[ps_t[6], esink_t], writes=[lnb_t[k]])
                op("act", lambda e: e.activation(out=lnb[k][:, :], in_=lnb[k][:, :], func=AF.Exp, scale=-1.0),
                   reads=[lnb_t[k]], writes=[lnb_t[k]])
                op("dve", lambda e: e.tensor_tensor(OT[0:64, slot, :], ps[bO][0:64, :], lnb[k][:, :], ALU.mult),
                   reads=[ps_t[bO], lnb_t[k]], writes=[OT_t[slot]])

            def stage1():
                for _, fn in pipe["defer"]:
                    fn()
                pipe["defer"] = []
                op("act", lambda e: e.activation(out=denrow[64:65, :], in_=ps[bO][64:65, :], func=AF.Copy),
                   reads=[ps_t[bO]], writes=[denrow_t])
                pipe["defer"].append((2, stage2))
            pipe["fifo"].append(("fin", stage1))

        def attn_windowed(j, i):
            lo = max(0, 4 * i - 1)
            hi = min(NB, 4 * i + 5)
            for g in range(4):
                kb = nxt("kb", 2)
                op("sp", lambda e, g=g, kb=kb: e.dma_start(out=kwin[kb][0:64, 0:(hi - lo) * 128], in_=KT_s[g, 0:64, lo * 128:hi * 128]),
                   reads=t_kv[max(0, i - 1):min(NT, i + 2)], writes=[kwin_t[kb]], dma=True)
                op("sp", lambda e, g=g, kb=kb: e.dma_start(out=vbuf[kb][:, 0:hi - lo, 0:64],
                                                           in_=V_s[g, :, lo * 64:hi * 64].rearrange("p (b e) -> p b e", e=64)),
                   reads=t_kv[max(0, i - 1):min(NT, i + 2)], writes=[vbuf_t[kb]], dma=True)
                for hh in range(3):
                    h = 3 * g + hh
                    bO = 4 + nxt("ob", 2)
                    blocks = list(range(lo, hi))
                    for n, jb in enumerate(blocks):
                        jj = jb - (4 * i - 1)
                        midx = jj
                        if jj == 0 and i == HNT:
                            midx = 6
                        if jj == 5 and i == HNT - 1:
                            midx = 7
                        kk = jb - lo
                        attn_unit(bO, n == 0, n == len(blocks) - 1,
                                  kwin[kb][:, kk * 128:(kk + 1) * 128], [kwin_t[kb]],
                                  headbuf[:, h, :], hb_t[h], 0.125, None, midx,
                                  vbuf[kb][:, kk, :], [vbuf_t[kb]])
                    attn_finish(bO, h, esink[:, j * 12 + h:j * 12 + h + 1])

        def attn_mla(j, i):
            scale = float(96 ** -0.5)
            th = i // HNT
            for h in range(12):
                bO = 4 + nxt("ob", 2)
                for qk in range(NKQ):
                    kb = nxt("kb", 2)
                    op("sp", lambda e, h=h, kb=kb, qk=qk: e.dma_start(out=kbuf[kb][0:96, 0:KCH], in_=KT_s[h, 0:96, qk * KCH:(qk + 1) * KCH]),
                       reads=t_kv, writes=[kbuf_t[kb]], dma=True)
                    op("sp", lambda e, h=h, kb=kb, qk=qk: e.dma_start(
                        out=vbuf[kb][:, 0:KCB, 0:64], in_=V_s[h, :, qk * KCB * 64:(qk + 1) * KCB * 64].rearrange("p (b e) -> p b e", e=64)),
                       reads=t_kv, writes=[vbuf_t[kb]], dma=True)
                    for kk in range(KCB):
                        kc = qk * KCB + kk
                        hk = (kc * 128) // HALF
                        bias = flags[:, 1:2] if hk != th else None
                        attn_unit(bO, kc == 0, kc == NB - 1,
                                  kbuf[kb][0:96, kk * 128:(kk + 1) * 128], [kbuf_t[kb]],
                                  headbuf[0:96, h, :], hb_t[h], scale, bias, None,
                                  vbuf[kb][:, kk, :], [vbuf_t[kb]])
                attn_finish(bO, h, None)

        def attn_mem(i):
            hf = i // HNT
            for m in range(4):
                bO = 4 + nxt("ob", 2)
                for ch in range(2):
                    attn_unit(bO, ch == 0, ch == 1, mk[:, hf, m, ch * 128:(ch + 1) * 128], [mkv_t],
                              headbuf[:, 12 + m, :], hb_t[12 + m], 0.125, None, None,
                              mv[:, hf, ch, m, :], [mkv_t])
                attn_finish(bO, 12 + m, None)

        def wo_apply(l):
            for c in range(DC):
                r = nxt("wob", 2)
                op("sp", lambda e, r=r, c=c: e.dma_start(out=wob[r][0:64, :], in_=wo_s[l, c]), reads=[t_wo[l]], writes=[wob_t[r]], dma=True)
                w3 = wob[r][:, :].rearrange("p (s d) -> p s d", s=16)
                b = 6 + (c % 2)
                for s_ in range(16):
                    op("pe", lambda e, s_=s_, w3=w3, b=b: e.matmul(ps[b][:, :], w3[:, s_, :], OT[:, s_, :],
                                                                   start=(s_ == 0), stop=(s_ == 15)),
                       reads=[wob_t[r], OT_t[s_]], writes=[ps_t[b]])
                op("dve", lambda e, c=c, b=b: e.tensor_tensor(xT[:, c, :], ps[b][:, :], xT[:, c, :], ALU.add),
                   reads=[ps_t[b], xT_t[c]], writes=[xT_t[c]])

        def per_tile(l):
            q = convq.get(l)
            if not q:
                return 0
            return (len(q) + 2 * NT - 2) // (2 * NT - 1) if not hasattr(per_tile, "c%d" % l) else getattr(per_tile, "c%d" % l)

        for l_ in list(convq.keys()):
            setattr(per_tile, "c%d" % l_, (len(convq[l_]) + 2 * NT - 2) // (2 * NT - 1))

        def main_schedule():
            for l in range(depth):
                j = l // 2
                isA = (l % 2 == 0)
                for tk_ in t_kv:
                    tk_.next_gen()
                for i in range(NT):
                    chk(4)
                    if l == 0:
                        load_x_input(i)
                    else:
                        load_x_tile(i)
                    chk(5)
                    ffn(2 * l, lambda c, l=l: g_ffn1(l, c))
                    chk(6)
                    store_x_tile(i)
                    rms_stats(lambda c: xT[:, c, :], xT_t, DC, TT, 1.0 / D)
                    rms_apply(lambda c: xT[:, c, :], xT_t, lambda c: hT[:, c, :], hT_t, DC, lambda c, l=l: g_mix(l, c))
                    if isA:
                        inproj_A(j, i)
                    else:
                        inproj_B(j, i)
                    conv_emit(l + 1, per_tile(l + 1))
                    chk(7)
                chk(8)
                mem_setup(l)
                chk(9)
                for i in range(NT):
                    nrow = 64 if isA else 96
                    op("sp", lambda e, i=i, nrow=nrow: e.dma_start(
                        out=headbuf[0:nrow, 0:12, :], in_=Q_s[0:12, 0:nrow, i * TT:(i + 1) * TT].rearrange("h r t -> r h t")),
                       reads=[t_q[i]], writes=hb_t[0:12], dma=True)
                    op("sp", lambda e, i=i: e.dma_start(
                        out=headbuf[0:64, 12:16, :], in_=Q_s[12:16, 0:64, i * TT:(i + 1) * TT].rearrange("h r t -> r h t")),
                       reads=[t_q[i]], writes=hb_t[12:16], dma=True)
                    if isA:
                        attn_windowed(j, i)
                    else:
                        attn_mla(j, i)
                    chk(10)
                    attn_mem(i)
                    pipe_flush()
                    load_x_tile(i)
                    chk(11)
                    wo_apply(l)
                    chk(12)
                    ffn(2 * l + 1, lambda c, l=l: g_ffn2(l, c))
                    if l == depth - 1:
                        final_out(i)
                    else:
                        store_x_tile(i)
                    conv_emit(l + 1, per_tile(l + 1) if i < NT - 1 else 10 ** 9)
                    chk(13)

        try:
            if stage > 3:
                main_schedule()
        except _Stop:
            pass

        print('sbuf bytes/partition', sb_bytes[0])
        S_.emit()
    return nc


def rope_tables_np(pos, dim):
    inv = 1.0 / (10000.0 ** (np.arange(0, dim, 2, dtype=np.float32) / np.float32(dim)))
    ang = pos.astype(np.float32)[:, None] * inv[None, :].astype(np.float32)
    return np.cos(ang).astype(np.float32), np.sin(ang).astype(np.float32)


def core_tables(S, two_seq):
    pos = np.arange(S)
    if two_seq:
        pos = pos % (S // 2)
    cA, sA = rope_tables_np(pos, 64)
    cB, sB = rope_tables_np(pos, 32)
    ropeA_c = np.concatenate([cA, cA], axis=1).T.copy()
    ropeA_s = np.concatenate([-sA, sA], axis=1).T.copy()
    ropeB_c = np.concatenate([np.ones((S, 64), np.float32), cB, cB], axis=1).T.copy()
    ropeB_s = np.concatenate([np.zeros((S, 64), np.float32), -sB, sB], axis=1).T.copy()
    flags = np.zeros((128, 2), np.float32)
    flags[:, 0] = 0.0 if two_seq else 1.0
    flags[:, 1] = -30000.0 if two_seq else 0.0
    return dict(ropeA_c=np.ascontiguousarray(ropeA_c, np.float32), ropeA_s=np.ascontiguousarray(ropeA_s, np.float32),
                ropeB_c=np.ascontiguousarray(ropeB_c, np.float32), ropeB_s=np.ascontiguousarray(ropeB_s, np.float32),
                flags=flags)


_PROG = {}

WNAMES = ["ffn1_norm", "ffn1_w_gu", "ffn1_w_down", "mix_norm", "mem_norm", "w_mem_kv", "a_w_in", "a_sink",
          "b_w_in", "b_q_norm", "b_w_q_up", "b_kv_norm", "b_w_kv_up", "w_o", "ffn2_norm", "ffn2_w_gu",
          "ffn2_w_down", "final_norm"]


def kernel(x_prompt, x_sample, mem_prompt, mem_sample, **weights):
    S = 8192
    x_prompt = np.asarray(x_prompt, np.float32)
    x_sample = np.asarray(x_sample, np.float32)
    mem_prompt = np.asarray(mem_prompt, np.float32)
    mem_sample = np.asarray(mem_sample, np.float32)
    wts = {k: np.ascontiguousarray(np.asarray(weights[k], np.float32)) for k in WNAMES}
    if S not in _PROG:
        _PROG[S] = build_program(S, 4)
    nc = _PROG[S]
    tp = core_tables(S, False)
    ts = core_tables(S, True)
    in_maps = []
    for c in range(8):
        m = dict(wts)
        if c < 2:
            m["x"] = np.ascontiguousarray(x_prompt[c])
            m["mem"] = np.ascontiguousarray(np.stack([mem_prompt[c], mem_prompt[c]]))
            m.update(tp)
        else:
            k = min(c - 2, 3)
            m["x"] = np.ascontiguousarray(x_sample[2 * k:2 * k + 2].reshape(S, D))
            m["mem"] = np.ascontiguousarray(mem_sample[2 * k:2 * k + 2])
            m.update(ts)
        in_maps.append(m)
    res = run_bass_kernel_spmd(nc, in_maps, core_ids=list(range(8)))
    y_prompt = np.stack([np.asarray(res.results[c]["y"], np.float32) for c in range(2)])
    y_sample = np.concatenate([np.asarray(res.results[2 + k]["y"], np.float32).reshape(2, S // 2, D) for k in range(4)], axis=0)
    return (y_prompt, y_sample)
```

```python
import contextlib
import numpy as np
import concourse.bass as bass
import concourse.mybir as mybir
from concourse.bass_utils import run_bass_kernel_spmd

F32 = mybir.dt.float32
BF16 = mybir.dt.bfloat16
AF = mybir.ActivationFunctionType
ALU = mybir.AluOpType

D = 1024
DC = 8
FF = 2816
FC = 22
TT = 512
NMEM = 256
EPS = 1e-6
ENGS = ("pe", "act", "dve", "pool", "sp")
N_DMA_SEMS = 40
DMA_SLOTS = {"sp": (0, 28), "pool": (28, 4), "act": (32, 8)}
EPOCH = 30000


class T:
    __slots__ = ("name", "lw", "rd")

    def __init__(self, name):
        self.name = name
        self.lw = None
        self.rd = []


def TL(name, n):
    return [T("%s%d" % (name, i)) for i in range(n)]


def _flat(x):
    out = []
    for t in x:
        if isinstance(t, (list, tuple)):
            out.extend(_flat(t))
        elif isinstance(t, DT):
            out.extend(t.cur)
        else:
            out.append(t)
    return out


class DT:
    def __init__(self, name):
        self.name = name
        self.cur = []
        self.old = []

    def w(self):
        t = T(self.name)
        self.cur.append(t)
        if self.old:
            o = self.old
            self.old = []
            return o + [t]
        return [t]

    def next_gen(self):
        self.old = self.old + self.cur
        self.cur = []


class Instr:
    __slots__ = ("eng", "idx", "fn", "deps", "is_dma", "dma_n", "signal", "cnt", "slot", "sval")

    def __init__(self, eng, idx, fn, is_dma):
        self.eng = eng
        self.idx = idx
        self.fn = fn
        self.deps = []
        self.is_dma = is_dma
        self.dma_n = None
        self.signal = False
        self.cnt = None


class Sched:
    def __init__(self, nc):
        self.nc = nc
        self.q = {e: [] for e in ENGS}
        self.n_dma = 0
        self.dmas = []
        self.seen_eng = {e: {} for e in ENGS}
        self.seen_dma = {e: set() for e in ENGS}
        self.dma_by_q = {}

    def op(self, eng, fn, reads=(), writes=(), dma=False):
        ins = Instr(eng, len(self.q[eng]), fn, dma)
        deps = {}

        def add(d, war=False):
            if d is None:
                return
            if d.eng == eng and not d.is_dma and not dma:
                if war or eng == "pe" or eng == "sp":
                    return
            if d.is_dma:
                if d.dma_n in self.seen_dma[eng]:
                    return
                deps[("dma", d.dma_n)] = d
            else:
                if self.seen_eng[eng].get(d.eng, -1) >= d.idx:
                    return
                if d.eng not in deps or deps[d.eng].idx < d.idx:
                    deps[d.eng] = d

        reads = _flat(reads)
        writes = _flat(writes)
        for t in reads:
            add(t.lw)
        for t in writes:
            add(t.lw)
            for r in t.rd:
                add(r, war=True)
        if dma:
            ins.dma_n = self.n_dma
            self.n_dma += 1
            self.dmas.append(ins)
            base, nsl = DMA_SLOTS[eng]
            lst = self.dma_by_q.setdefault(eng, [])
            k = len(lst)
            ins.slot = base + (k % nsl)
            ins.sval = 16 * (k // nsl + 1)
            if k >= nsl:
                prev = lst[k - nsl]
                if prev.dma_n not in self.seen_dma[eng]:
                    deps[("dma", prev.dma_n)] = prev
            lst.append(ins)
        ins.deps = list(deps.values())
        for d in ins.deps:
            d.signal = True
            if d.is_dma:
                self.seen_dma[eng].add(d.dma_n)
            else:
                self.seen_eng[eng][d.eng] = d.idx
        for t in reads:
            t.rd.append(ins)
        for t in writes:
            t.lw = ins
            t.rd = []
        self.q[eng].append(ins)
        return ins

    def emit(self):
        nc = self.nc
        nep = {}
        for e in ENGS:
            c = 0
            for ins in self.q[e]:
                if not ins.is_dma and ins.signal:
                    ins.cnt = c
                    c += 1
            nep[e] = max(1, (c + EPOCH - 1) // EPOCH)
        with contextlib.ExitStack() as st:
            sems = {e: [st.enter_context(nc.semaphore("s_%s%d" % (e, i))) for i in range(nep[e])]
                    for e in ENGS if e != "sp"}
            dsems = [st.enter_context(nc.semaphore("d%d" % i)) for i in range(N_DMA_SEMS)]
            block = st.enter_context(nc.Block())
            engobj = {"pe": block.tensor, "act": block.scalar, "dve": block.vector,
                      "pool": block.gpsimd, "sp": block.sync}

            def mk(e):
                def body(eng):
                    for ins in self.q[e]:
                        for d in ins.deps:
                            if d.is_dma:
                                eng.wait_ge(dsems[d.slot], d.sval)
                            else:
                                eng.wait_ge(sems[d.eng][d.cnt // EPOCH], d.cnt % EPOCH + 1)
                        r = ins.fn(eng)
                        if ins.is_dma:
                            r.then_inc(dsems[ins.slot], 16)
                        elif ins.signal:
                            r.then_inc(sems[ins.eng][ins.cnt // EPOCH], 1)
                    if e == "sp":
                        last = {}
                        for d in self.dmas:
                            last[d.slot] = d
                        for k, d in last.items():
                            eng.wait_ge(dsems[k], d.sval)
                return body

            for e in ENGS:
                engobj[e](mk(e))


class _Stop(Exception):
    pass


def build_program(S, depth=4, stage=99):
    def chk(n):
        if n >= stage:
            raise _Stop()
    NT = S // TT
    NB = S // 128
    HALF = S // 2
    HNT = NT // 2
    KCH = min(2048, HALF)
    NKQ = S // KCH
    KCB = KCH // 128
    nA = (depth + 1) // 2
    nB = depth // 2

    nc = bass.Bass("TRN2", target_bir_lowering=False)

    def din(name, shape, dt=F32):
        return nc.dram_tensor(name, list(shape), dt, kind="ExternalInput").ap()

    def dscr(name, shape, dt):
        return nc.dram_tensor(name, list(shape), dt, kind="Internal").ap()

    x_in = din("x", [S, D])
    mem_in = din("mem", [2, NMEM, D])
    flags_in = din("flags", [128, 2])
    ropeA_c = din("ropeA_c", [64, S])
    ropeA_s = din("ropeA_s", [64, S])
    ropeB_c = din("ropeB_c", [96, S])
    ropeB_s = din("ropeB_s", [96, S])
    W = {}
    W["ffn1_norm"] = din("ffn1_norm", [4, D])
    W["ffn1_w_gu"] = din("ffn1_w_gu", [4, D, 2 * FF])
    W["ffn1_w_down"] = din("ffn1_w_down", [4, FF, D])
    W["mix_norm"] = din("mix_norm", [4, D])
    W["mem_norm"] = din("mem_norm", [4, D])
    W["w_mem_kv"] = din("w_mem_kv", [4, D, 512])
    W["a_w_in"] = din("a_w_in", [2, D, 1536])
    W["a_sink"] = din("a_sink", [2, 12])
    W["b_w_in"] = din("b_w_in", [2, D, 928])
    W["b_q_norm"] = din("b_q_norm", [2, 384])
    W["b_w_q_up"] = din("b_w_q_up", [2, 384, 1152])
    W["b_kv_norm"] = din("b_kv_norm", [2, 256])
    W["b_w_kv_up"] = din("b_w_kv_up", [2, 256, 1536])
    W["w_o"] = din("w_o", [4, D, D])
    W["ffn2_norm"] = din("ffn2_norm", [4, D])
    W["ffn2_w_gu"] = din("ffn2_w_gu", [4, D, 2 * FF])
    W["ffn2_w_down"] = din("ffn2_w_down", [4, FF, D])
    W["final_norm"] = din("final_norm", [D])
    y_out = nc.dram_tensor("y", [S, D], F32, kind="ExternalOutput").ap()

    wgu_s = dscr("wgu_s", [8, 22, 128, 2048], BF16)
    wd_s = dscr("wd_s", [8, 8, 128, FC * 128], BF16)
    wpa_s = dscr("wpa_s", [2, 10, 128, 2048], BF16)
    wpb_s = dscr("wpb_s", [2, 4, 128, 2048], BF16)
    wqu_s = dscr("wqu_s", [2, 4, 128, 2048], BF16)
    wku_s = dscr("wku_s", [2, 128, 2048], BF16)
    wvu_s = dscr("wvu_s", [2, 128, 2048], BF16)
    wo_s = dscr("wo_s", [4, 8, 64, 2048], BF16)
    wmem_s = dscr("wmem_s", [4, 2, 128, 2048], BF16)
    KT_s = dscr("KT_s", [12, 96, S], BF16)
    V_s = dscr("V_s", [12, 128, NB * 64], BF16)
    Q_s = dscr("Q_s", [16, 96, S], BF16)
    xs = dscr("xs", [NT, 128, 4096], F32)

    S_ = Sched(nc)
    _cap = {"on": False, "q": None}

    def op(*a, **k):
        if _cap["on"]:
            _cap["q"].append((a, k))
            return None
        return S_.op(*a, **k)

    with contextlib.ExitStack() as st:
        sb_bytes = [0]

        def sb(name, shape, dt):
            n = 1
            for d_ in shape[1:]:
                n *= d_
            sb_bytes[0] += n * (4 if dt == F32 else 2)
            return st.enter_context(nc.sbuf_tensor("sb_" + name, list(shape), dt))

        def pst(name, shape, dt):
            return st.enter_context(nc.psum_tensor("pp_" + name, list(shape), dt))

        xT = sb("xT", [128, DC, TT], F32)
        xT_t = TL("xT", DC)
        hT = sb("hT", [128, DC, TT], BF16)
        hT_t = TL("hT", DC)
        actT = sb("actT", [128, FC * TT], BF16)
        actT_t = TL("actT", FC)
        actT3 = actT[:, :].rearrange("p (f t) -> p f t", f=FC)
        xin = actT[:, 0:8192].bitcast(F32).rearrange("p (b d) -> p b d", b=4)
        scrF = actT[:, 0:FC * TT].bitcast(F32)
        gu = [sb("gu%d" % i, [128, 2048], BF16) for i in range(4)]
        gu_t = TL("gu", 4)
        wd = [sb("wd%d" % i, [128, FC * 128], BF16) for i in range(3)]
        wd_t = TL("wd", 3)
        wr = [sb("wr%d" % i, [128, 2048], BF16) for i in range(3)]
        wr_t = TL("wr", 3)
        tmpf = [sb("tmpf%d" % i, [128, TT], F32) for i in range(4)]
        tmpf_t = TL("tmpf", 4)
        sq = [sb("sq%d" % i, [128, TT], BF16) for i in range(2)]
        sq_t = TL("sq", 2)
        rstd = sb("rstd", [128, TT], F32)
        rstd_t = T("rstd")
        headbuf = sb("headbuf", [128, 16, TT], BF16)
        hb_t = TL("hb", 16)
        OT = sb("OT", [128, 16, TT], BF16)
        OT_t = TL("OT", 16)
        kbuf = [sb("kbuf%d" % i, [128, 2048], BF16) for i in range(2)]
        kbuf_t = TL("kbuf", 2)
        vbuf = [sb("vbuf%d" % i, [128, 16, 128], BF16) for i in range(2)]
        kwin = [sb("kwin%d" % i, [128, 768], BF16) for i in range(2)]
        kwin_t = TL("kwin", 2)
        wob = [sb("wob%d" % i, [128, 2048], BF16) for i in range(2)]
        wob_t = TL("wob", 2)
        vbuf_t = TL("vbuf", 2)
        pt = [sb("pt%d" % i, [128, TT], BF16) for i in range(4)]
        pt_t = TL("pt", 4)
        masks = sb("masks", [128, 8, TT], BF16)
        masks_t = T("masks")
        vstage = sb("vstage", [128, 12, 4, 64], BF16)
        vst_t = T("vstage")
        krope = sb("krope", [32, TT], BF16)
        krope_t = T("krope")
        gA = sb("gA", [128, 128], F32)
        gB = sb("gB", [128, 128], F32)
        gT = sb("gT", [128, 256], F32)
        g_t = T("gains")
        ident = sb("ident", [128, 128], F32)
        ident_t = T("ident")
        ones_bf = sb("ones_bf", [128, 128], BF16)
        sel_f = sb("sel_f", [128, 64], F32)
        ones_t = T("ones")
        denrow = sb("denrow", [128, TT], F32)
        denrow_t = T("denrow")
        lnb = [sb("lnb%d" % i, [64, TT], F32) for i in range(2)]
        lnb_t = TL("lnb", 2)
        esink = sb("esink", [64, 24], F32)
        esink_t = T("esink")
        flags = sb("flags", [128, 2], F32)
        flags_t = T("flags")
        rc = sb("rc", [96, TT], F32)
        rs = sb("rs", [96, TT], F32)
        rc32 = sb("rc32", [32, TT], F32)
        rs32 = sb("rs32", [32, TT], F32)
        rope_t = T("rope")
        mk = sb("mk", [128, 2, 4, NMEM], BF16)
        mv = sb("mv", [128, 2, 2, 4, 128], BF16)
        mkv_t = T("mkv")
        ps = [pst("ps%d" % i, [128, TT], F32) for i in range(8)]
        ps_t = TL("ps", 8)

        t_wffn = [DT("wffn") for _ in range(8)]
        t_wpa = [DT("wpa") for _ in range(2)]
        t_wpb = [DT("wpb") for _ in range(2)]
        t_wo = [DT("wo") for _ in range(4)]
        t_wmem = [DT("wmem") for _ in range(4)]
        t_xs = TL("xs", NT)
        t_kv = [DT("kvs") for _ in range(NT)]
        t_q = TL("qs", NT)

        _dbg_try = True
        op("pool", lambda e: e.memset(ident[:, :], 0.0), writes=[ident_t])
        op("pool", lambda e: e.affine_select(out=ident[:, :], in_=ident[:, :], pattern=[[-1, 128]],
                                             compare_op=ALU.not_equal, fill=1.0, base=0,
                                             channel_multiplier=1), reads=[ident_t], writes=[ident_t])
        op("pool", lambda e: e.memset(ones_bf[:, :], 1.0), writes=[ones_t])
        op("pool", lambda e: e.memset(sel_f[:, :], 0.0), writes=[ones_t])
        op("pool", lambda e: e.memset(sel_f[64:65, :], 1.0), writes=[ones_t])
        op("pool", lambda e: e.memset(denrow[:, :], 0.0), writes=[denrow_t])
        op("pool", lambda e: e.memset(mv[:, :, :, :, 64:128], 1.0), writes=[mkv_t])
        op("pool", lambda e: e.memset(mk[:, :, :, :], 0.0), writes=[mkv_t])
        for i_ in range(2):
            op("pool", lambda e, i_=i_: e.memset(vbuf[i_][:, :, 64:128], 1.0), writes=[vbuf_t[i_]])
            op("pool", lambda e, i_=i_: e.memset(kwin[i_][:, :], 0.0), writes=[kwin_t[i_]])
            op("pool", lambda e, i_=i_: e.memset(wob[i_][:, :], 0.0), writes=[wob_t[i_]])
        op("pool", lambda e: e.memset(headbuf[:, :, :], 0.0), writes=hb_t)
        op("pool", lambda e: e.memset(OT[:, :, :], 0.0), writes=OT_t)
        op("pool", lambda e: e.memset(wr[0][:, :], 0.0), writes=[wr_t[0]])
        op("pool", lambda e: e.memset(gA[:, :], 0.0), writes=[g_t])
        op("pool", lambda e: e.memset(gB[:, :], 0.0), writes=[g_t])
        op("sp", lambda e: e.dma_start(out=flags[:, :], in_=flags_in[:, :]), writes=[flags_t], dma=True)
        op("pool", lambda e: e.memset(masks[:, :, :], 1.0), writes=[masks_t])
        for jj in range(6):
            op("pool", lambda e, jj=jj: e.affine_select(
                out=masks[:, jj, :], in_=masks[:, jj, :], pattern=[[-1, TT]], compare_op=ALU.is_ge,
                fill=0.0, base=128 + (jj - 1) * 128, channel_multiplier=1), reads=[masks_t], writes=[masks_t])
            op("pool", lambda e, jj=jj: e.affine_select(
                out=masks[:, jj, :], in_=masks[:, jj, :], pattern=[[1, TT]], compare_op=ALU.is_ge,
                fill=0.0, base=128 - (jj - 1) * 128, channel_multiplier=-1), reads=[masks_t], writes=[masks_t])
        op("pool", lambda e: e.tensor_scalar(masks[:, 6, :], masks[:, 0, :], flags[:, 0:1], None, ALU.mult),
           reads=[masks_t, flags_t], writes=[masks_t])
        op("pool", lambda e: e.tensor_scalar(masks[:, 7, :], masks[:, 5, :], flags[:, 0:1], None, ALU.mult),
           reads=[masks_t, flags_t], writes=[masks_t])
        for gi, nm in enumerate(["ffn1_norm", "mix_norm", "mem_norm", "ffn2_norm"]):
            op("sp", lambda e, gi=gi, nm=nm: e.dma_start(
                out=gA[gi * 32:(gi + 1) * 32, :], in_=W[nm][:, :].rearrange("l (c p) -> (l c) p", p=128)),
               writes=[g_t], dma=True)
        op("sp", lambda e: e.dma_start(out=gB[0:8, :], in_=W["final_norm"].rearrange("(c p) -> c p", p=128)),
           writes=[g_t], dma=True)
        op("sp", lambda e: e.dma_start(out=gB[8:14, :], in_=W["b_q_norm"][:, :].rearrange("l (c p) -> (l c) p", p=128)),
           writes=[g_t], dma=True)
        op("sp", lambda e: e.dma_start(out=gB[14:18, :], in_=W["b_kv_norm"][:, :].rearrange("l (c p) -> (l c) p", p=128)),
           writes=[g_t], dma=True)
        op("pe", lambda e: e.transpose(ps[6][:, 0:128], gA[:, :], ident[:, :]), reads=[g_t, ident_t], writes=[ps_t[6]])
        op("pe", lambda e: e.transpose(ps[6][:, 128:256], gB[:, :], ident[:, :]), reads=[g_t, ident_t], writes=[ps_t[6]])
        gT_t = T("gT")
        op("act", lambda e: e.activation(out=gT[:, :], in_=ps[6][:, 0:256], func=AF.Copy), reads=[ps_t[6]], writes=[gT_t])

        def g_ffn1(l, c): return gT[:, l * 8 + c: l * 8 + c + 1]
        def g_mix(l, c): return gT[:, 32 + l * 8 + c: 32 + l * 8 + c + 1]
        def g_mem(l, c): return gT[:, 64 + l * 8 + c: 64 + l * 8 + c + 1]
        def g_ffn2(l, c): return gT[:, 96 + l * 8 + c: 96 + l * 8 + c + 1]
        def g_fin(c): return gT[:, 128 + c: 128 + c + 1]
        def g_bq(j, c): return gT[:, 136 + j * 3 + c: 136 + j * 3 + c + 1]
        def g_bkv(j, c): return gT[:, 142 + j * 2 + c: 142 + j * 2 + c + 1]

        op("sp", lambda e: e.dma_start(out=esink[:, :], in_=W["a_sink"][:, :].rearrange("a b -> (a b)").partition_broadcast(64)),
           writes=[esink_t], dma=True)
        op("act", lambda e: e.activation(out=esink[:, :], in_=esink[:, :], func=AF.Exp), reads=[esink_t], writes=[esink_t])

        def cvt(out_ap, in_ap, tw, extra_reads=()):
            op("pool", lambda e: e.dma_start(out=out_ap, in_=in_ap), reads=list(extra_reads), writes=tw.w(), dma=True)

        def cvt_ffn(k):
            l = k // 2
            wg = W["ffn1_w_gu" if k % 2 == 0 else "ffn2_w_gu"][l]
            wdn = W["ffn1_w_down" if k % 2 == 0 else "ffn2_w_down"][l]
            for f in range(FC):
                dst = wgu_s[k, f].rearrange("p (c e) -> p c e", c=8)
                cvt(dst[:, :, 0:128], wg[:, f * 128:(f + 1) * 128].rearrange("(c p) e -> p c e", p=128), t_wffn[k])
                cvt(dst[:, :, 128:256], wg[:, FF + f * 128:FF + (f + 1) * 128].rearrange("(c p) e -> p c e", p=128), t_wffn[k])
            for c in range(8):
                cvt(wd_s[k, c].rearrange("p (f d) -> p f d", f=FC),
                    wdn[:, c * 128:(c + 1) * 128].rearrange("(f p) d -> p f d", p=128), t_wffn[k])

        def cvt_cols(dst3, src2d, c0, n, d0=0):
            return (dst3[:, :, d0:d0 + n], src2d[:, c0:c0 + n].rearrange("(c p) f -> p c f", p=128))

        def cvt_swapped(dst3, src2d, c0, nheads, hd, tw, d0=0):
            h2 = hd // 2
            for c in range(dst3.shape[1]):
                d4 = dst3[:, c, d0:d0 + nheads * hd].rearrange("p (h e) -> p h e", e=hd)
                s4 = src2d[c * 128:(c + 1) * 128, c0:c0 + nheads * hd].rearrange("p (h e) -> p h e", e=hd)
                cvt(d4[:, :, 0:h2], s4[:, :, h2:hd], tw)
                cvt(d4[:, :, h2:hd], s4[:, :, 0:h2], tw)

        def cvt_layerA(j):
            w = W["a_w_in"][j]
            tw = t_wpa[j]
            def dst(pc): return wpa_s[j, pc].rearrange("p (c f) -> p c f", c=8)
            for hg in range(3):
                cvt(*cvt_cols(dst(hg), w, hg * 256, 256), tw)
                cvt_swapped(dst(3 + hg), w, hg * 256, 4, 64, tw)
            cvt(*cvt_cols(dst(6), w, 768, 256), tw)
            cvt_swapped(dst(7), w, 768, 4, 64, tw)
            cvt(*cvt_cols(dst(8), w, 1024, 256), tw)
            cvt(*cvt_cols(dst(9), w, 1280, 256), tw)


        def cvt_layerB(j):
            w = W["b_w_in"][j]
            tw = t_wpb[j]
            def dst(pc): return wpb_s[j, pc].rearrange("p (c f) -> p c f", c=8)
            cvt(*cvt_cols(dst(0), w, 0, 256), tw)
            cvt(*cvt_cols(dst(1), w, 256, 128), tw)
            cvt(*cvt_cols(dst(1), w, 640, 32, d0=128), tw)
            cvt_swapped(dst(1), w, 640, 1, 32, tw, d0=160)
            cvt(*cvt_cols(dst(2), w, 384, 256), tw)
            cvt(*cvt_cols(dst(3), w, 672, 256), tw)
            wq = W["b_w_q_up"][j]
            for pc in range(2):
                d3 = wqu_s[j, pc][:, 0:1728].rearrange("p (c f) -> p c f", c=3)
                cvt(d3, wq[:, pc * 576:(pc + 1) * 576].rearrange("(c p) f -> p c f", p=128), tw)
                d3s = wqu_s[j, 2 + pc][:, 0:1728].rearrange("p (c f) -> p c f", c=3)
                zw = tw.w()
                op("pool", lambda e, dz=wqu_s[j, 2 + pc][:, 0:1728]: e.dma_start(out=dz, in_=zero_sb[:, 0:1728]),
                   reads=[zeros_t], writes=zw, dma=True)
                for c in range(3):
                    d4 = d3s[:, c, :].rearrange("p (h e) -> p h e", e=96)
                    s4 = wq[c * 128:(c + 1) * 128, pc * 576:(pc + 1) * 576].rearrange("p (h e) -> p h e", e=96)
                    cvt(d4[:, :, 64:80], s4[:, :, 80:96], tw, extra_reads=zw[-1:])
                    cvt(d4[:, :, 80:96], s4[:, :, 64:80], tw, extra_reads=zw[-1:])
            wkv = W["b_w_kv_up"][j]
            dk = wku_s[j][:, 0:1536].rearrange("p (c h e) -> p c h e", c=2, h=12)
            dv = wvu_s[j][:, 0:1536].rearrange("p (c h e) -> p c h e", c=2, h=12)
            for c in range(2):
                s4 = wkv[c * 128:(c + 1) * 128, :].rearrange("p (h e) -> p h e", e=128)
                cvt(dk[:, c, :, :], s4[:, :, 0:64], tw)
                cvt(dv[:, c, :, :], s4[:, :, 64:128], tw)

        def cvt_wo(l):
            for c in range(8):
                cvt(wo_s[l, c].rearrange("r (s d) -> r s d", s=16),
                    W["w_o"][l][:, c * 128:(c + 1) * 128].rearrange("(s r) d -> r s d", r=64), t_wo[l])

        def cvt_mem(l):
            for hv in range(2):
                cvt(wmem_s[l, hv].rearrange("p (c f) -> p c f", c=8),
                    W["w_mem_kv"][l][:, hv * 256:(hv + 1) * 256].rearrange("(c p) f -> p c f", p=128), t_wmem[l])

        zero_sb = wr[0]
        zeros_t = wr_t[0]

        convq = {}
        for l in range(depth):
            if l > 0:
                _cap["on"] = True
                _cap["q"] = []
            cvt_ffn(2 * l)
            if l % 2 == 0:
                cvt_layerA(l // 2)
            else:
                cvt_layerB(l // 2)
            cvt_mem(l)
            cvt_wo(l)
            cvt_ffn(2 * l + 1)
            if l > 0:
                convq[l] = _cap["q"]
                _cap["on"] = False

        def conv_emit(l, n):
            q = convq.get(l)
            while q and n > 0:
                a_, k_ = q.pop(0)
                S_.op(*a_, **k_)
                n -= 1

        rot = {"wob": 0, "sq": 0, "tmp": 0, "gu": 0, "wd": 0, "wr": 0, "pj": 0, "pt": 0, "st": 0, "kb": 0, "ob": 0, "ln": 0, "tr": 0}

        def nxt(key, n):
            v = rot[key]
            rot[key] = (v + 1) % n
            return v

        def rms_stats(src_fn, src_ts, nchunk, width, inv_n):
            for c in range(nchunk):
                k = nxt("sq", 2)
                if c % 2 == 0:
                    op("act", lambda e, c=c, k=k: e.activation(out=sq[k][:, 0:width], in_=src_fn(c), func=AF.Square),
                       reads=[src_ts[c]], writes=[sq_t[k]])
                else:
                    op("dve", lambda e, c=c, k=k: e.tensor_tensor(sq[k][:, 0:width], src_fn(c), src_fn(c), ALU.mult),
                       reads=[src_ts[c]], writes=[sq_t[k]])
                op("pe", lambda e, c=c, k=k: e.matmul(ps[6][:, 0:width], ones_bf[:, :], sq[k][:, 0:width],
                                                      start=(c == 0), stop=(c == nchunk - 1)),
                   reads=[sq_t[k], ones_t], writes=[ps_t[6]])
            op("dve", lambda e: e.tensor_scalar(rstd[:, 0:width], ps[6][:, 0:width], inv_n, EPS, ALU.mult, ALU.add),
               reads=[ps_t[6]], writes=[rstd_t])
            op("act", lambda e: e.activation(out=rstd[:, 0:width], in_=rstd[:, 0:width], func=AF.Ln),
               reads=[rstd_t], writes=[rstd_t])
            op("act", lambda e: e.activation(out=rstd[:, 0:width], in_=rstd[:, 0:width], func=AF.Exp, scale=-0.5),
               reads=[rstd_t], writes=[rstd_t])

        def rms_apply(src_fn, src_ts, dst_fn, dst_ts, nchunk, gfn):
            for c in range(nchunk):
                op("dve", lambda e, c=c: e.scalar_tensor_tensor(dst_fn(c), src_fn(c), gfn(c), rstd_of(src_fn(c)),
                                                               ALU.mult, ALU.mult),
                   reads=[src_ts[c], rstd_t, gT_t], writes=[dst_ts[c]])

        def rstd_of(ap):
            return rstd[:, 0:ap.shape[-1]]

        def ffn(k, gfn):
            rms_stats(lambda c: xT[:, c, :], xT_t, DC, TT, 1.0 / D)
            rms_apply(lambda c: xT[:, c, :], xT_t, lambda c: hT[:, c, :], hT_t, DC, gfn)
            for f in range(FC):
                r = nxt("gu", 4)
                op("sp", lambda e, r=r, f=f: e.dma_start(out=gu[r][:, :], in_=wgu_s[k, f]),
                   reads=[t_wffn[k]], writes=[gu_t[r]], dma=True)
                g3 = gu[r][:, :].rearrange("p (c e) -> p c e", c=8)
                bg = (f % 2) * 2
                bu = bg + 1
                for c in range(DC):
                    op("pe", lambda e, c=c, g3=g3, bg=bg: e.matmul(
                        ps[bg][:, :], g3[:, c, 0:128], hT[:, c, :], start=(c == 0), stop=(c == DC - 1)),
                       reads=[gu_t[r], hT_t[c]], writes=[ps_t[bg]])
                for c in range(DC):
                    op("pe", lambda e, c=c, g3=g3, bu=bu: e.matmul(
                        ps[bu][:, :], g3[:, c, 128:256], hT[:, c, :], start=(c == 0), stop=(c == DC - 1)),
                       reads=[gu_t[r], hT_t[c]], writes=[ps_t[bu]])
                tk = nxt("tmp", 4)
                op("act", lambda e, tk=tk, bg=bg: e.activation(out=tmpf[tk][:, :], in_=ps[bg][:, :], func=AF.Silu),
                   reads=[ps_t[bg]], writes=[tmpf_t[tk]])
                op("dve", lambda e, tk=tk, bu=bu, f=f: e.tensor_tensor(actT3[:, f, :], tmpf[tk][:, :], ps[bu][:, :], ALU.mult),
                   reads=[tmpf_t[tk], ps_t[bu]], writes=[actT_t[f]])
            for c in range(DC):
                r = nxt("wd", 3)
                op("sp", lambda e, r=r, c=c: e.dma_start(out=wd[r][:, :], in_=wd_s[k, c]),
                   reads=[t_wffn[k]], writes=[wd_t[r]], dma=True)
                w3 = wd[r][:, :].rearrange("p (f d) -> p f d", f=FC)
                b = 4 + (c % 2)
                for f in range(FC):
                    op("pe", lambda e, f=f, w3=w3, b=b: e.matmul(ps[b][:, :], w3[:, f, :], actT3[:, f, :],
                                                                 start=(f == 0), stop=(f == FC - 1)),
                       reads=[wd_t[r], actT_t[f]], writes=[ps_t[b]])
                op("dve", lambda e, c=c, b=b: e.scalar_tensor_tensor(xT[:, c, :], ps[b][:, :], 0.5, xT[:, c, :], ALU.mult, ALU.add),
                   reads=[ps_t[b], xT_t[c]], writes=[xT_t[c]])

        def load_wr(src_ap, tw, rows=128, n=2048):
            r = nxt("wr", 3)
            op("sp", lambda e: e.dma_start(out=wr[r][0:rows, 0:n], in_=src_ap), reads=[tw], writes=[wr_t[r]], dma=True)
            return r

        def pj_bank():
            return 4 + nxt("pj", 4)

        def proj(bank, rows, lhs_fn, rhs_fn, nk, reads, width=TT):
            for c in range(nk):
                op("pe", lambda e, c=c: e.matmul(ps[bank][0:rows, 0:width], lhs_fn(c), rhs_fn(c),
                                                 start=(c == 0), stop=(c == nk - 1)),
                   reads=reads(c), writes=[ps_t[bank]])

        def rope_combine(bA, bB, rows, ctab, stab, dst_ap, dst_t):
            t1 = nxt("tmp", 4)
            t2 = nxt("tmp", 4)
            op("dve", lambda e: e.tensor_tensor(tmpf[t1][0:rows, :], ps[bA][0:rows, :], ctab, ALU.mult),
               reads=[ps_t[bA], rope_t], writes=[tmpf_t[t1]])
            op("dve", lambda e: e.tensor_tensor(tmpf[t2][0:rows, :], ps[bB][0:rows, :], stab, ALU.mult),
               reads=[ps_t[bB], rope_t], writes=[tmpf_t[t2]])
            op("pool", lambda e: e.tensor_tensor(dst_ap, tmpf[t1][0:rows, :], tmpf[t2][0:rows, :], ALU.add),
               reads=[tmpf_t[t1], tmpf_t[t2]], writes=[dst_t])

        def load_x_tile(i):
            op("sp", lambda e: e.dma_start(out=xT[:, :, :], in_=xs[i].rearrange("p (c t) -> p c t", c=DC)),
               reads=[t_xs[i]], writes=xT_t, dma=True)

        def store_x_tile(i):
            op("pool", lambda e: e.dma_start(out=xs[i].rearrange("p (c t) -> p c t", c=DC), in_=xT[:, :, :]),
               reads=xT_t, writes=[t_xs[i]], dma=True)

        def load_x_input(i):
            op("sp", lambda e: e.dma_start(out=xin[:, :, :], in_=x_in[i * TT:(i + 1) * TT, :].rearrange("(b p) d -> p b d", p=128)),
               writes=actT_t[0:16], dma=True)
            for c in range(DC):
                b = 6 + nxt("tr", 2)
                for blk in range(4):
                    op("pe", lambda e, c=c, blk=blk, b=b: e.transpose(ps[b][:, blk * 128:(blk + 1) * 128],
                                                                      xin[:, blk, c * 128:(c + 1) * 128], ident[:, :]),
                       reads=actT_t[0:16] + [ident_t], writes=[ps_t[b]])
                op("act", lambda e, c=c, b=b: e.activation(out=xT[:, c, :], in_=ps[b][:, :], func=AF.Copy),
                   reads=[ps_t[b]], writes=[xT_t[c]])

        def final_out(i):
            rms_stats(lambda c: xT[:, c, :], xT_t, DC, TT, 1.0 / D)
            rms_apply(lambda c: xT[:, c, :], xT_t, lambda c: xT[:, c, :], xT_t, DC, g_fin)
            for blk in range(4):
                for cg in range(2):
                    b = 6 + nxt("tr", 2)
                    for cc in range(4):
                        c = cg * 4 + cc
                        op("pe", lambda e, c=c, cc=cc, blk=blk, b=b: e.transpose(
                            ps[b][:, cc * 128:(cc + 1) * 128], xT[:, c, blk * 128:(blk + 1) * 128], ident[:, :]),
                           reads=[xT_t[c], ident_t], writes=[ps_t[b]])
                    op("act", lambda e, blk=blk, cg=cg, b=b: e.activation(out=xin[:, blk, cg * 512:(cg + 1) * 512], in_=ps[b][:, :], func=AF.Copy),
                       reads=[ps_t[b]], writes=actT_t[0:16])
            op("pool", lambda e: e.dma_start(out=y_out[i * TT:(i + 1) * TT, :].rearrange("(b p) d -> p b d", p=128), in_=xin[:, :, :]),
               reads=actT_t[0:16], dma=True)

        def load_rope(i, layerA):
            cs = slice(i * TT, (i + 1) * TT)
            if layerA:
                op("sp", lambda e: e.dma_start(out=rc[0:64, :], in_=ropeA_c[:, cs]), writes=[rope_t], dma=True)
                op("sp", lambda e: e.dma_start(out=rs[0:64, :], in_=ropeA_s[:, cs]), writes=[rope_t], dma=True)
            else:
                op("sp", lambda e: e.dma_start(out=rc[:, :], in_=ropeB_c[:, cs]), writes=[rope_t], dma=True)
                op("sp", lambda e: e.dma_start(out=rs[:, :], in_=ropeB_s[:, cs]), writes=[rope_t], dma=True)
                op("sp", lambda e: e.dma_start(out=rc32[:, :], in_=ropeB_c[64:96, cs]), writes=[rope_t], dma=True)
                op("sp", lambda e: e.dma_start(out=rs32[:, :], in_=ropeB_s[64:96, cs]), writes=[rope_t], dma=True)

        def inproj_A(j, i):
            cs = slice(i * TT, (i + 1) * TT)
            tw = t_wpa[j]
            load_rope(i, True)
            for grp in range(4):
                rN = load_wr(wpa_s[j, grp if grp < 3 else 6], tw)
                rS = load_wr(wpa_s[j, 3 + grp if grp < 3 else 7], tw)
                wN = wr[rN][:, :].rearrange("p (c f) -> p c f", c=8)
                wS = wr[rS][:, :].rearrange("p (c f) -> p c f", c=8)
                for hh in range(4):
                    bA = pj_bank()
                    bB = pj_bank()
                    proj(bA, 64, lambda c, hh=hh, wN=wN: wN[:, c, hh * 64:(hh + 1) * 64], lambda c: hT[:, c, :], DC,
                         lambda c, rN=rN: [wr_t[rN], hT_t[c]])
                    proj(bB, 64, lambda c, hh=hh, wS=wS: wS[:, c, hh * 64:(hh + 1) * 64], lambda c: hT[:, c, :], DC,
                         lambda c, rS=rS: [wr_t[rS], hT_t[c]])
                    if grp < 3:
                        h = grp * 4 + hh
                        rope_combine(bA, bB, 64, rc[0:64, :], rs[0:64, :], headbuf[0:64, h, :], hb_t[h])
                    else:
                        rope_combine(bA, bB, 64, rc[0:64, :], rs[0:64, :], OT[0:64, hh, :], OT_t[hh])
            rV = load_wr(wpa_s[j, 8], tw)
            wV = wr[rV][:, :].rearrange("p (c f) -> p c f", c=8)
            for blk in range(4):
                b = pj_bank()
                proj(b, 128, lambda c, blk=blk: hT[:, c, blk * 128:(blk + 1) * 128], lambda c, wV=wV: wV[:, c, 0:256], DC,
                     lambda c, rV=rV: [wr_t[rV], hT_t[c]], width=256)
                op("act", lambda e, b=b, blk=blk: e.activation(out=vstage[:, 0:4, blk, 0:64],
                                                               in_=ps[b][:, 0:256].rearrange("p (g e) -> p g e", g=4), func=AF.Copy),
                   reads=[ps_t[b]], writes=[vst_t])
            rQ = load_wr(wpa_s[j, 9], tw)
            wQ = wr[rQ][:, :].rearrange("p (c f) -> p c f", c=8)
            for m in range(4):
                b = pj_bank()
                proj(b, 64, lambda c, m=m, wQ=wQ: wQ[:, c, m * 64:(m + 1) * 64], lambda c: hT[:, c, :], DC,
                     lambda c, rQ=rQ: [wr_t[rQ], hT_t[c]])
                op("act", lambda e, b=b, m=m: e.activation(out=headbuf[0:64, 12 + m, :], in_=ps[b][0:64, :], func=AF.Copy),
                   reads=[ps_t[b]], writes=[hb_t[12 + m]])
            op("pool", lambda e: e.dma_start(out=Q_s[0:16, 0:64, cs].rearrange("h r t -> r h t"), in_=headbuf[0:64, 0:16, :]),
               reads=hb_t, writes=[t_q[i]], dma=True)
            op("pool", lambda e: e.dma_start(out=KT_s[0:4, 0:64, cs].rearrange("h r t -> r h t"), in_=OT[0:64, 0:4, :]),
               reads=OT_t[0:4], writes=t_kv[i].w(), dma=True)
            op("pool", lambda e: e.dma_start(
                out=V_s[0:4, :, i * 256:(i + 1) * 256].rearrange("h p (b e) -> p h b e", b=4),
                in_=vstage[:, 0:4, :, :]), reads=[vst_t], writes=t_kv[i].w(), dma=True)

        def inproj_B(j, i):
            cs = slice(i * TT, (i + 1) * TT)
            tw = t_wpb[j]
            load_rope(i, False)
            cq = scrF[:, 0:1536].rearrange("p (c t) -> p c t", c=3)
            ckv = scrF[:, 1536:2560].rearrange("p (c t) -> p c t", c=2)
            A_ALL = actT_t
            r0 = load_wr(wpb_s[j, 0], tw)
            w0 = wr[r0][:, :].rearrange("p (c f) -> p c f", c=8)
            r1 = load_wr(wpb_s[j, 1], tw)
            w1 = wr[r1][:, :].rearrange("p (c f) -> p c f", c=8)
            for cc in range(3):
                b = pj_bank()
                if cc < 2:
                    proj(b, 128, lambda c, cc=cc: w0[:, c, cc * 128:(cc + 1) * 128], lambda c: hT[:, c, :], DC,
                         lambda c: [wr_t[r0], hT_t[c]])
                else:
                    proj(b, 128, lambda c: w1[:, c, 0:128], lambda c: hT[:, c, :], DC, lambda c: [wr_t[r1], hT_t[c]])
                op("act", lambda e, b=b, cc=cc: e.activation(out=cq[:, cc, :], in_=ps[b][:, :], func=AF.Copy),
                   reads=[ps_t[b]], writes=A_ALL)
            bA = pj_bank()
            bB = pj_bank()
            proj(bA, 32, lambda c: w1[:, c, 128:160], lambda c: hT[:, c, :], DC, lambda c: [wr_t[r1], hT_t[c]])
            proj(bB, 32, lambda c: w1[:, c, 160:192], lambda c: hT[:, c, :], DC, lambda c: [wr_t[r1], hT_t[c]])
            rope_combine(bA, bB, 32, rc32[:, :], rs32[:, :], krope[:, :], krope_t)
            r2 = load_wr(wpb_s[j, 2], tw)
            w2 = wr[r2][:, :].rearrange("p (c f) -> p c f", c=8)
            for cc in range(2):
                b = pj_bank()
                proj(b, 128, lambda c, cc=cc: w2[:, c, cc * 128:(cc + 1) * 128], lambda c: hT[:, c, :], DC,
                     lambda c: [wr_t[r2], hT_t[c]])
                op("act", lambda e, b=b, cc=cc: e.activation(out=ckv[:, cc, :], in_=ps[b][:, :], func=AF.Copy),
                   reads=[ps_t[b]], writes=A_ALL)
            r3 = load_wr(wpb_s[j, 3], tw)
            w3 = wr[r3][:, :].rearrange("p (c f) -> p c f", c=8)
            for m in range(4):
                b = pj_bank()
                proj(b, 64, lambda c, m=m: w3[:, c, m * 64:(m + 1) * 64], lambda c: hT[:, c, :], DC,
                     lambda c: [wr_t[r3], hT_t[c]])
                op("act", lambda e, b=b, m=m: e.activation(out=headbuf[0:64, 12 + m, :], in_=ps[b][0:64, :], func=AF.Copy),
                   reads=[ps_t[b]], writes=[hb_t[12 + m]])
            rms_stats(lambda c: cq[:, c, :], [A_ALL] * 3, 3, TT, 1.0 / 384)
            rms_apply(lambda c: cq[:, c, :], [A_ALL] * 3, lambda c: hT[:, c, :], hT_t[0:3], 3, lambda c: g_bq(j, c))
            rms_stats(lambda c: ckv[:, c, :], [A_ALL] * 2, 2, TT, 1.0 / 256)
            rms_apply(lambda c: ckv[:, c, :], [A_ALL] * 2, lambda c: hT[:, 4 + c, :], hT_t[4:6], 2, lambda c: g_bkv(j, c))
            for pc in range(2):
                rN = load_wr(wqu_s[j, pc], tw)
                rS = load_wr(wqu_s[j, 2 + pc], tw)
                wN = wr[rN][:, 0:1728].rearrange("p (c f) -> p c f", c=3)
                wS = wr[rS][:, 0:1728].rearrange("p (c f) -> p c f", c=3)
                for hh in range(6):
                    h = pc * 6 + hh
                    bA = pj_bank()
                    bB = pj_bank()
                    proj(bA, 96, lambda c, hh=hh, wN=wN: wN[:, c, hh * 96:(hh + 1) * 96], lambda c: hT[:, c, :], 3,
                         lambda c, rN=rN: [wr_t[rN], hT_t[c]])
                    proj(bB, 96, lambda c, hh=hh, wS=wS: wS[:, c, hh * 96:(hh + 1) * 96], lambda c: hT[:, c, :], 3,
                         lambda c, rS=rS: [wr_t[rS], hT_t[c]])
                    rope_combine(bA, bB, 96, rc[:, :], rs[:, :], headbuf[0:96, h, :], hb_t[h])
            rK = load_wr(wku_s[j], tw)
            wK = wr[rK][:, 0:1536].rearrange("p (c f) -> p c f", c=2)
            for h in range(12):
                b = pj_bank()
                proj(b, 64, lambda c, h=h: wK[:, c, h * 64:(h + 1) * 64], lambda c: hT[:, 4 + c, :], 2,
                     lambda c: [wr_t[rK], hT_t[4 + c]])
                op("act", lambda e, b=b, h=h: e.activation(out=OT[0:64, h, :], in_=ps[b][0:64, :], func=AF.Copy),
                   reads=[ps_t[b]], writes=[OT_t[h]])
            rV = load_wr(wvu_s[j], tw)
            wV = wr[rV][:, 0:1536].rearrange("p (c f) -> p c f", c=2)
            for blk in range(4):
                b = pj_bank()
                proj(b, 128, lambda c, blk=blk: hT[:, 4 + c, blk * 128:(blk + 1) * 128], lambda c: wV[:, c, 0:512], 2,
                     lambda c: [wr_t[rV], hT_t[4 + c]])
                op("act", lambda e, b=b, blk=blk: e.activation(out=vstage[:, 0:8, blk, 0:64],
                                                               in_=ps[b][:, 0:512].rearrange("p (g e) -> p g e", g=8), func=AF.Copy),
                   reads=[ps_t[b]], writes=[vst_t])
                b = pj_bank()
                proj(b, 128, lambda c, blk=blk: hT[:, 4 + c, blk * 128:(blk + 1) * 128], lambda c: wV[:, c, 512:768], 2,
                     lambda c: [wr_t[rV], hT_t[4 + c]], width=256)
                op("act", lambda e, b=b, blk=blk: e.activation(out=vstage[:, 8:12, blk, 0:64],
                                                               in_=ps[b][:, 0:256].rearrange("p (g e) -> p g e", g=4), func=AF.Copy),
                   reads=[ps_t[b]], writes=[vst_t])
            op("pool", lambda e: e.dma_start(out=Q_s[0:12, 0:96, cs].rearrange("h r t -> r h t"), in_=headbuf[0:96, 0:12, :]),
               reads=hb_t[0:12], writes=[t_q[i]], dma=True)
            op("pool", lambda e: e.dma_start(out=Q_s[12:16, 0:64, cs].rearrange("h r t -> r h t"), in_=headbuf[0:64, 12:16, :]),
               reads=hb_t[12:16], writes=[t_q[i]], dma=True)
            op("pool", lambda e: e.dma_start(out=KT_s[0:12, 0:64, cs].rearrange("h r t -> r h t"), in_=OT[0:64, 0:12, :]),
               reads=OT_t[0:12], writes=t_kv[i].w(), dma=True)
            for h in range(12):
                op("pool", lambda e, h=h: e.dma_start(out=KT_s[h, 64:96, cs], in_=krope[:, :]),
                   reads=[krope_t], writes=t_kv[i].w(), dma=True)
            op("pool", lambda e: e.dma_start(
                out=V_s[0:12, :, i * 256:(i + 1) * 256].rearrange("h p (b e) -> p h b e", b=4),
                in_=vstage[:, :, :, :]), reads=[vst_t], writes=t_kv[i].w(), dma=True)

        def mem_setup(l):
            r = nxt("gu", 4)
            r2 = nxt("gu", 4)
            op("sp", lambda e: e.dma_start(out=gu[r][:, :], in_=wmem_s[l, 0]), reads=[t_wmem[l]], writes=[gu_t[r]], dma=True)
            op("sp", lambda e: e.dma_start(out=gu[r2][:, :], in_=wmem_s[l, 1]), reads=[t_wmem[l]], writes=[gu_t[r2]], dma=True)
            wm = gu[r][:, :].rearrange("p (c f) -> p c f", c=8)
            wm2 = gu[r2][:, :].rearrange("p (c f) -> p c f", c=8)
            for hf in range(2):
                op("sp", lambda e, hf=hf: e.dma_start(out=xin[:, 0:2, :], in_=mem_in[hf].rearrange("(b p) d -> p b d", p=128)),
                   writes=actT_t[0:16], dma=True)
                for c in range(DC):
                    b = 6 + nxt("tr", 2)
                    for blk in range(2):
                        op("pe", lambda e, c=c, blk=blk, b=b: e.transpose(ps[b][:, blk * 128:(blk + 1) * 128],
                                                                          xin[:, blk, c * 128:(c + 1) * 128], ident[:, :]),
                           reads=actT_t[0:16] + [ident_t], writes=[ps_t[b]])
                    op("act", lambda e, c=c, b=b: e.activation(out=xT[:, c, 0:NMEM], in_=ps[b][:, 0:NMEM], func=AF.Copy),
                       reads=[ps_t[b]], writes=[xT_t[c]])
                rms_stats(lambda c: xT[:, c, 0:NMEM], xT_t, DC, NMEM, 1.0 / D)
                rms_apply(lambda c: xT[:, c, 0:NMEM], xT_t, lambda c: hT[:, c, 0:NMEM], hT_t, DC, lambda c: g_mem(l, c))
                for m in range(4):
                    b = pj_bank()
                    proj(b, 64, lambda c, m=m: wm[:, c, m * 64:(m + 1) * 64], lambda c: hT[:, c, 0:NMEM], DC,
                         lambda c: [gu_t[r], hT_t[c]], width=NMEM)
                    op("act", lambda e, b=b, m=m, hf=hf: e.activation(out=mk[0:64, hf, m, :], in_=ps[b][0:64, 0:NMEM], func=AF.Copy),
                       reads=[ps_t[b]], writes=[mkv_t])
                for ch in range(2):
                    b = pj_bank()
                    proj(b, 128, lambda c, ch=ch: hT[:, c, ch * 128:(ch + 1) * 128], lambda c: wm2[:, c, 0:256], DC,
                         lambda c: [gu_t[r2], hT_t[c]], width=256)
                    op("act", lambda e, b=b, ch=ch, hf=hf: e.activation(out=mv[:, hf, ch, :, 0:64],
                                                                       in_=ps[b][:, 0:256].rearrange("p (g e) -> p g e", g=4), func=AF.Copy),
                       reads=[ps_t[b]], writes=[mkv_t])

        LOOK = 3
        pipe = {"fifo": [], "defer": []}

        def pipe_pop(limit):
            while pipe["fifo"]:
                npv = sum(1 for k, _ in pipe["fifo"] if k == "pv")
                if pipe["fifo"][0][0] == "fin" or npv > limit:
                    pipe["fifo"].pop(0)[1]()
                    continue
                break

        def pipe_tick():
            nd = []
            for cnt, fn in pipe["defer"]:
                if cnt <= 1:
                    fn()
                else:
                    nd.append((cnt - 1, fn))
            pipe["defer"] = nd

        def pipe_flush():
            pipe_pop(0)
            while pipe["fifo"]:
                pipe["fifo"].pop(0)[1]()
            for _, fn in pipe["defer"]:
                fn()
            pipe["defer"] = []

        def attn_unit(bO, first, last, klhs, klhs_reads, qrhs, q_t, scale, bias_ap, mask_idx, vlhs, v_reads):
            bS = nxt("st", 4)
            op("pe", lambda e: e.matmul(ps[bS][:, :], klhs, qrhs, start=True, stop=True),
               reads=klhs_reads + [q_t], writes=[ps_t[bS]])
            p = nxt("pt", 4)
            if bias_ap is None:
                op("act", lambda e: e.activation(out=pt[p][:, :], in_=ps[bS][:, :], func=AF.Exp, scale=scale),
                   reads=[ps_t[bS]], writes=[pt_t[p]])
            else:
                op("act", lambda e: e.activation(out=pt[p][:, :], in_=ps[bS][:, :], func=AF.Exp, scale=scale, bias=bias_ap),
                   reads=[ps_t[bS], flags_t], writes=[pt_t[p]])
            if mask_idx is not None:
                op("dve", lambda e: e.tensor_tensor(pt[p][:, :], pt[p][:, :], masks[:, mask_idx, :], ALU.mult),
                   reads=[pt_t[p], masks_t], writes=[pt_t[p]])
            pipe_tick()

            def pv():
                op("pe", lambda e: e.matmul(ps[bO][:, :], vlhs, pt[p][:, :], start=first, stop=last),
                   reads=v_reads + [pt_t[p]], writes=[ps_t[bO]])
            pipe["fifo"].append(("pv", pv))
            pipe_pop(LOOK)

        def attn_finish(bO, slot, sink_ap):
            def stage2():
                op("pe", lambda e: e.matmul(ps[6][0:64, :], sel_f[:, :], denrow[:, :], start=True, stop=True),
                   reads=[denrow_t, ones_t], writes=[ps_t[6]])
                k = nxt("ln", 2)
                if sink_ap is None:
                    op("act", lambda e: e.activation(out=lnb[k][:, :], in_=ps[6][0:64, :], func=AF.Ln),
                       reads=[ps_t[6]], writes=[lnb_t[k]])
                else:
                    op("act", lambda e: e.activation(out=lnb[k][:, :], in_=ps[6][0:64, :], func=AF.Ln, bias=sink_ap),
                       reads=[ps_t[6], esink_t], writes=[lnb_t[k]])
                op("act", lambda e: e.activation(out=lnb[k][:, :], in_=lnb[k][:, :], func=AF.Exp, scale=-1.0),
                   reads=[lnb_t[k]], writes=[lnb_t[k]])
                op("dve", lambda e: e.tensor_tensor(OT[0:64, slot, :], ps[bO][0:64, :], lnb[k][:, :], ALU.mult),
                   reads=[ps_t[bO], lnb_t[k]], writes=[OT_t[slot]])

            def stage1():
                for _, fn in pipe["defer"]:
                    fn()
                pipe["defer"] = []
                op("act", lambda e: e.activation(out=denrow[64:65, :], in_=ps[bO][64:65, :], func=AF.Copy),
                   reads=[ps_t[bO]], writes=[denrow_t])
                pipe["defer"].append((2, stage2))
            pipe["fifo"].append(("fin", stage1))

        def attn_windowed(j, i):
            lo = max(0, 4 * i - 1)
            hi = min(NB, 4 * i + 5)
            for g in range(4):
                kb = nxt("kb", 2)
                op("sp", lambda e, g=g, kb=kb: e.dma_start(out=kwin[kb][0:64, 0:(hi - lo) * 128], in_=KT_s[g, 0:64, lo * 128:hi * 128]),
                   reads=t_kv[max(0, i - 1):min(NT, i + 2)], writes=[kwin_t[kb]], dma=True)
                op("sp", lambda e, g=g, kb=kb: e.dma_start(out=vbuf[kb][:, 0:hi - lo, 0:64],
                                                           in_=V_s[g, :, lo * 64:hi * 64].rearrange("p (b e) -> p b e", e=64)),
                   reads=t_kv[max(0, i - 1):min(NT, i + 2)], writes=[vbuf_t[kb]], dma=True)
                for hh in range(3):
                    h = 3 * g + hh
                    bO = 4 + nxt("ob", 2)
                    blocks = list(range(lo, hi))
                    for n, jb in enumerate(blocks):
                        jj = jb - (4 * i - 1)
                        midx = jj
                        if jj == 0 and i == HNT:
                            midx = 6
                        if jj == 5 and i == HNT - 1:
                            midx = 7
                        kk = jb - lo
                        attn_unit(bO, n == 0, n == len(blocks) - 1,
                                  kwin[kb][:, kk * 128:(kk + 1) * 128], [kwin_t[kb]],
                                  headbuf[:, h, :], hb_t[h], 0.125, None, midx,
                                  vbuf[kb][:, kk, :], [vbuf_t[kb]])
                    attn_finish(bO, h, esink[:, j * 12 + h:j * 12 + h + 1])

        def attn_mla(j, i):
            scale = float(96 ** -0.5)
            th = i // HNT
            for h in range(12):
                bO = 4 + nxt("ob", 2)
                for qk in range(NKQ):
                    kb = nxt("kb", 2)
                    op("sp", lambda e, h=h, kb=kb, qk=qk: e.dma_start(out=kbuf[kb][0:96, 0:KCH], in_=KT_s[h, 0:96, qk * KCH:(qk + 1) * KCH]),
                       reads=t_kv, writes=[kbuf_t[kb]], dma=True)
                    op("sp", lambda e, h=h, kb=kb, qk=qk: e.dma_start(
                        out=vbuf[kb][:, 0:KCB, 0:64], in_=V_s[h, :, qk * KCB * 64:(qk + 1) * KCB * 64].rearrange("p (b e) -> p b e", e=64)),
                       reads=t_kv, writes=[vbuf_t[kb]], dma=True)
                    for kk in range(KCB):
                        kc = qk * KCB + kk
                        hk = (kc * 128) // HALF
                        bias = flags[:, 1:2] if hk != th else None
                        attn_unit(bO, kc == 0, kc == NB - 1,
                                  kbuf[kb][0:96, kk * 128:(kk + 1) * 128], [kbuf_t[kb]],
                                  headbuf[0:96, h, :], hb_t[h], scale, bias, None,
                                  vbuf[kb][:, kk, :], [vbuf_t[kb]])
                attn_finish(bO, h, None)

        def attn_mem(i):
            hf = i // HNT
            for m in range(4):
                bO = 4 + nxt("ob", 2)
                for ch in range(2):
                    attn_unit(bO, ch == 0, ch == 1, mk[:, hf, m, ch * 128:(ch + 1) * 128], [mkv_t],
                              headbuf[:, 12 + m, :], hb_t[12 + m], 0.125, None, None,
                              mv[:, hf, ch, m, :], [mkv_t])
                attn_finish(bO, 12 + m, None)

        def wo_apply(l):
            for c in range(DC):
                r = nxt("wob", 2)
                op("sp", lambda e, r=r, c=c: e.dma_start(out=wob[r][0:64, :], in_=wo_s[l, c]), reads=[t_wo[l]], writes=[wob_t[r]], dma=True)
                w3 = wob[r][:, :].rearrange("p (s d) -> p s d", s=16)
                b = 6 + (c % 2)
                for s_ in range(16):
                    op("pe", lambda e, s_=s_, w3=w3, b=b: e.matmul(ps[b][:, :], w3[:, s_, :], OT[:, s_, :],
                                                                   start=(s_ == 0), stop=(s_ == 15)),
                       reads=[wob_t[r], OT_t[s_]], writes=[ps_t[b]])
                op("dve", lambda e, c=c, b=b: e.tensor_tensor(xT[:, c, :], ps[b][:, :], xT[:, c, :], ALU.add),
                   reads=[ps_t[b], xT_t[c]], writes=[xT_t[c]])

        def per_tile(l):
            q = convq.get(l)
            if not q:
                return 0
            return (len(q) + 2 * NT - 2) // (2 * NT - 1) if not hasattr(per_tile, "c%d" % l) else getattr(per_tile, "c%d" % l)

        for l_ in list(convq.keys()):
            setattr(per_tile, "c%d" % l_, (len(convq[l_]) + 2 * NT - 2) // (2 * NT - 1))

        def main_schedule():
            for l in range(depth):
                j = l // 2
                isA = (l % 2 == 0)
                for tk_ in t_kv:
                    tk_.next_gen()
                for i in range(NT):
                    chk(4)
                    if l == 0:
                        load_x_input(i)
                    else:
                        load_x_tile(i)
                    chk(5)
                    ffn(2 * l, lambda c, l=l: g_ffn1(l, c))
                    chk(6)
                    store_x_tile(i)
                    rms_stats(lambda c: xT[:, c, :], xT_t, DC, TT, 1.0 / D)
                    rms_apply(lambda c: xT[:, c, :], xT_t, lambda c: hT[:, c, :], hT_t, DC, lambda c, l=l: g_mix(l, c))
                    if isA:
                        inproj_A(j, i)
                    else:
                        inproj_B(j, i)
                    conv_emit(l + 1, per_tile(l + 1))
                    chk(7)
                chk(8)
                mem_setup(l)
                chk(9)
                for i in range(NT):
                    nrow = 64 if isA else 96
                    op("sp", lambda e, i=i, nrow=nrow: e.dma_start(
                        out=headbuf[0:nrow, 0:12, :], in_=Q_s[0:12, 0:nrow, i * TT:(i + 1) * TT].rearrange("h r t -> r h t")),
                       reads=[t_q[i]], writes=hb_t[0:12], dma=True)
                    op("sp", lambda e, i=i: e.dma_start(
                        out=headbuf[0:64, 12:16, :], in_=Q_s[12:16, 0:64, i * TT:(i + 1) * TT].rearrange("h r t -> r h t")),
                       reads=[t_q[i]], writes=hb_t[12:16], dma=True)
                    if isA:
                        attn_windowed(j, i)
                    else:
                        attn_mla(j, i)
                    chk(10)
                    attn_mem(i)
                    pipe_flush()
                    load_x_tile(i)
                    chk(11)
                    wo_apply(l)
                    chk(12)
                    ffn(2 * l + 1, lambda c, l=l: g_ffn2(l, c))
                    if l == depth - 1:
                        final_out(i)
                    else:
                        store_x_tile(i)
                    conv_emit(l + 1, per_tile(l + 1) if i < NT - 1 else 10 ** 9)
                    chk(13)

        try:
            if stage > 3:
                main_schedule()
        except _Stop:
            pass

        print('sbuf bytes/partition', sb_bytes[0])
        S_.emit()
    return nc


def rope_tables_np(pos, dim):
    inv = 1.0 / (10000.0 ** (np.arange(0, dim, 2, dtype=np.float32) / np.float32(dim)))
    ang = pos.astype(np.float32)[:, None] * inv[None, :].astype(np.float32)
    return np.cos(ang).astype(np.float32), np.sin(ang).astype(np.float32)


def core_tables(S, two_seq):
    pos = np.arange(S)
    if two_seq:
        pos = pos % (S // 2)
    cA, sA = rope_tables_np(pos, 64)
    cB, sB = rope_tables_np(pos, 32)
    ropeA_c = np.concatenate([cA, cA], axis=1).T.copy()
    ropeA_s = np.concatenate([-sA, sA], axis=1).T.copy()
    ropeB_c = np.concatenate([np.ones((S, 64), np.float32), cB, cB], axis=1).T.copy()
    ropeB_s = np.concatenate([np.zeros((S, 64), np.float32), -sB, sB], axis=1).T.copy()
    flags = np.zeros((128, 2), np.float32)
    flags[:, 0] = 0.0 if two_seq else 1.0
    flags[:, 1] = -30000.0 if two_seq else 0.0
    return dict(ropeA_c=np.ascontiguousarray(ropeA_c, np.float32), ropeA_s=np.ascontiguousarray(ropeA_s, np.float32),
                ropeB_c=np.ascontiguousarray(ropeB_c, np.float32), ropeB_s=np.ascontiguousarray(ropeB_s, np.float32),
                flags=flags)


_PROG = {}

WNAMES = ["ffn1_norm", "ffn1_w_gu", "ffn1_w_down", "mix_norm", "mem_norm", "w_mem_kv", "a_w_in", "a_sink",
          "b_w_in", "b_q_norm", "b_w_q_up", "b_kv_norm", "b_w_kv_up", "w_o", "ffn2_norm", "ffn2_w_gu",
          "ffn2_w_down", "final_norm"]


def kernel(x_prompt, x_sample, mem_prompt, mem_sample, **weights):
    S = 8192
    x_prompt = np.asarray(x_prompt, np.float32)
    x_sample = np.asarray(x_sample, np.float32)
    mem_prompt = np.asarray(mem_prompt, np.float32)
    mem_sample = np.asarray(mem_sample, np.float32)
    wts = {k: np.ascontiguousarray(np.asarray(weights[k], np.float32)) for k in WNAMES}
    if S not in _PROG:
        _PROG[S] = build_program(S, 4)
    nc = _PROG[S]
    tp = core_tables(S, False)
    ts = core_tables(S, True)
    in_maps = []
    for c in range(8):
        m = dict(wts)
        if c < 2:
            m["x"] = np.ascontiguousarray(x_prompt[c])
            m["mem"] = np.ascontiguousarray(np.stack([mem_prompt[c], mem_prompt[c]]))
            m.update(tp)
        else:
            k = min(c - 2, 3)
            m["x"] = np.ascontiguousarray(x_sample[2 * k:2 * k + 2].reshape(S, D))
            m["mem"] = np.ascontiguousarray(mem_sample[2 * k:2 * k + 2])
            m.update(ts)
        in_maps.append(m)
    res = run_bass_kernel_spmd(nc, in_maps, core_ids=list(range(8)))
    y_prompt = np.stack([np.asarray(res.results[c]["y"], np.float32) for c in range(2)])
    y_sample = np.concatenate([np.asarray(res.results[2 + k]["y"], np.float32).reshape(2, S // 2, D) for k in range(4)], axis=0)
    return (y_prompt, y_sample)
```
